# Optimizing a Trainium2 kernel written in Bass

```python
import math
import jax, jax.numpy as jnp
from jax import lax
import numpy as np

D_MODEL = 4096
BATCH = 2
SEQ = 8192
DEPTH = 2

N_MIXERS = 2
RMS_EPS = 1e-6

DILATED_GROUPS = ((128, 1), (512, 4), (2048, 16))
N_GROUPS = 3
ATTN_HEAD_DIM = 128
ATTN_HEADS_PER_GROUP = 16
ATTN_HEADS = N_GROUPS * ATTN_HEADS_PER_GROUP
ATTN_BLOCK = 128
NEG_INF = -1e30

NUM_BUCKETS = 32
MAX_EXACT = 16
REL_MAX_DISTANCE = 2048

HG_HEAD_DIM = 128
HG_HEADS = D_MODEL // HG_HEAD_DIM
HG_CHUNK = 64

D_FF = -(-8 * D_MODEL // (3 * 256)) * 256

kernel_name = 'hybrid_dilated_attn_hgrn2_swiglu'


def rms_norm(x, gain):
    xf = x.astype(jnp.float32)
    y = xf * lax.rsqrt(jnp.mean(xf * xf, axis=-1, keepdims=True) + RMS_EPS)
    return (y * gain.astype(jnp.float32)).astype(x.dtype)


def t5_causal_bucket(dist):
    dist = jnp.maximum(dist, 0)
    log_ratio = jnp.log(jnp.maximum(dist, 1).astype(jnp.float32) / MAX_EXACT) / math.log(REL_MAX_DISTANCE / MAX_EXACT)
    large = MAX_EXACT + (log_ratio * (NUM_BUCKETS - MAX_EXACT)).astype(jnp.int32)
    large = jnp.minimum(large, NUM_BUCKETS - 1)
    return jnp.where(dist < MAX_EXACT, dist, large)


def dilated_window_group(q, k, v, rel_bias_g, window, dilation):
    B, S, H, Dh = q.shape
    span = window // dilation
    n = S // dilation
    nb = -(-n // ATTN_BLOCK)
    n_pad = nb * ATTN_BLOCK

    def to_blocks(a):
        a = a.reshape(B, n, dilation, H, Dh).transpose(0, 3, 2, 1, 4)
        a = jnp.pad(a, ((0, 0), (0, 0), (0, 0), (0, n_pad - n), (0, 0)))
        return a.reshape(B, H, dilation, nb, ATTN_BLOCK, Dh)

    def with_prev(a):
        prev = jnp.pad(a, ((0, 0), (0, 0), (0, 0), (1, 0), (0, 0), (0, 0)))[:, :, :, :-1]
        return jnp.concatenate([prev, a], axis=-2)

    qb = to_blocks(q)
    kc = with_prev(to_blocks(k))
    vc = with_prev(to_blocks(v))

    qi = jnp.arange(ATTN_BLOCK)[:, None]
    ki = jnp.arange(2 * ATTN_BLOCK)[None, :]
    rel = qi - ki + ATTN_BLOCK
    band = (rel >= 0) & (rel <= span)
    valid = band[None] & ((jnp.arange(nb)[:, None, None] > 0) | (ki[None] >= ATTN_BLOCK))
    bias = rel_bias_g[t5_causal_bucket(rel * dilation)].astype(jnp.float32)
    bias = bias.transpose(2, 0, 1)[None, :, None, None]

    logits = jnp.einsum('bhrnqd,bhrnkd->bhrnqk', qb, kc, preferred_element_type=jnp.float32) * (Dh ** -0.5) + bias
    logits = jnp.where(valid, logits, NEG_INF)
    lse = jax.nn.logsumexp(logits, axis=-1)
    probs = jnp.exp(logits - lse[..., None])
    out = jnp.einsum('bhrnqk,bhrnkd->bhrnqd', probs.astype(v.dtype), vc, preferred_element_type=jnp.float32)
    out = out.reshape(B, H, dilation, n_pad, Dh)[:, :, :, :n].transpose(0, 3, 2, 1, 4).reshape(B, S, H, Dh)
    lse = lse.reshape(B, H, dilation, n_pad)[..., :n].transpose(0, 3, 2, 1).reshape(B, S, H)
    return out, lse


def dilated_attention_mixer(y, w_in, w_out, rel_bias):
    B, S, _ = y.shape
    proj = (y @ w_in).reshape(B, S, N_GROUPS, 3, ATTN_HEADS_PER_GROUP, ATTN_HEAD_DIM)
    outs, lses = [], []
    for g, (window, dilation) in enumerate(DILATED_GROUPS):
        cols = rel_bias[:, g * ATTN_HEADS_PER_GROUP:(g + 1) * ATTN_HEADS_PER_GROUP]
        o, l = dilated_window_group(proj[:, :, g, 0], proj[:, :, g, 1], proj[:, :, g, 2], cols, window, dilation)
        outs.append(o)
        lses.append(l)
    alpha = jax.nn.softmax(jnp.stack(lses), axis=0)
    merged = jnp.sum(alpha[..., None] * jnp.stack(outs), axis=0)
    return merged.reshape(B, S, ATTN_HEADS_PER_GROUP * ATTN_HEAD_DIM).astype(y.dtype) @ w_out


def hgrn2_chunk_recurrence(q, k, v, log_f):
    B, H, S, Dk = q.shape
    Dv = v.shape[-1]
    nc = S // HG_CHUNK
    mid = HG_CHUNK // 2

    def chunks(a):
        return a.reshape(B, H, nc, HG_CHUNK, a.shape[-1]).transpose(2, 0, 1, 3, 4)

    causal = jnp.tril(jnp.ones((HG_CHUNK, HG_CHUNK), dtype=bool))

    def step(state, inp):
        qc, kc, vc, gc = inp
        b = jnp.cumsum(gc, axis=-2)
        b_ref = b[..., mid - 1:mid, :]
        b_last = b[..., -1:, :]
        inter = jnp.einsum('bhtd,bhde->bhte', qc * jnp.exp(b), state)
        scores = jnp.einsum('bhtd,bhsd->bhts', qc * jnp.exp(b - b_ref), kc * jnp.exp(b_ref - b))
        intra = jnp.einsum('bhts,bhse->bhte', jnp.where(causal, scores, 0.0), vc)
        new_state = state * jnp.exp(b_last[..., 0, :])[..., None] + jnp.einsum('bhsd,bhse->bhde', kc * jnp.exp(b_last - b), vc)
        return new_state, inter + intra

    state0 = jnp.zeros((B, H, Dk, Dv), jnp.float32)
    _, out = lax.scan(step, state0, (chunks(q), chunks(k), chunks(v), chunks(log_f)))
    return out.transpose(1, 2, 0, 3, 4).reshape(B, H, S, Dv)


def hgrn2_mixer(y, w_in, lower_bound, out_gain, w_out):
    B, S, _ = y.shape
    proj = (y @ w_in).reshape(B, S, 4, HG_HEADS, HG_HEAD_DIM).astype(jnp.float32)
    q_raw, f_raw, i_raw, g_raw = proj[:, :, 0], proj[:, :, 1], proj[:, :, 2], proj[:, :, 3]
    lb = lower_bound.astype(jnp.float32).reshape(HG_HEADS, HG_HEAD_DIM)
    forget = lb + (1.0 - lb) * jax.nn.sigmoid(f_raw)
    log_f = jnp.log(forget)
    key = 1.0 - forget
    q = jax.nn.silu(q_raw) * (HG_HEAD_DIM ** -0.5)
    to_bhsd = lambda a: a.transpose(0, 2, 1, 3)
    o = hgrn2_chunk_recurrence(to_bhsd(q), to_bhsd(key), to_bhsd(i_raw), to_bhsd(log_f))
    o = rms_norm(o, out_gain).transpose(0, 2, 1, 3) * jax.nn.silu(g_raw)
    return o.reshape(B, S, HG_HEADS * HG_HEAD_DIM).astype(y.dtype) @ w_out


def swiglu_ffn(y, w_in, w_out):
    h = y @ w_in
    gate, up = h[..., :D_FF], h[..., D_FF:]
    return (jax.nn.silu(gate) * up) @ w_out


def setup_inputs(seed: int = 0) -> dict:
    key = jax.random.key(seed)
    ks = jax.random.split(key, 11)
    n_attn = (DEPTH + 1) // 2
    n_hg = DEPTH // 2
    attn_in_cols = N_GROUPS * 3 * ATTN_HEADS_PER_GROUP * ATTN_HEAD_DIM
    attn_out_rows = ATTN_HEADS_PER_GROUP * ATTN_HEAD_DIM
    hg_width = HG_HEADS * HG_HEAD_DIM

    def dense(k, shape, fan_in):
        return jax.random.normal(k, shape, jnp.float32) * (fan_in ** -0.5)

    return {
        'x': jax.random.normal(ks[0], (BATCH, SEQ, D_MODEL), jnp.float32),
        'norm_gains': 1.0 + 0.05 * jax.random.normal(ks[1], (DEPTH, 4, D_MODEL), jnp.float32),
        'rel_bias': 0.1 * jax.random.normal(ks[2], (NUM_BUCKETS, ATTN_HEADS), jnp.float32),
        'attn_w_in': dense(ks[3], (n_attn, D_MODEL, attn_in_cols), D_MODEL),
        'attn_w_out': dense(ks[4], (n_attn, attn_out_rows, D_MODEL), attn_out_rows),
        'hgrn_w_in': dense(ks[5], (n_hg, D_MODEL, 4 * hg_width), D_MODEL),
        'hgrn_lb_logits': 1.0 + 0.1 * jax.random.normal(ks[6], (DEPTH, hg_width), jnp.float32),
        'hgrn_out_gain': 1.0 + 0.05 * jax.random.normal(ks[7], (n_hg, HG_HEAD_DIM), jnp.float32),
        'hgrn_w_out': dense(ks[8], (n_hg, hg_width, D_MODEL), hg_width),
        'ffn_w_in': dense(ks[9], (DEPTH, D_MODEL, 2 * D_FF), D_MODEL),
        'ffn_w_out': dense(ks[10], (DEPTH, D_FF, D_MODEL), D_FF),
    }


def reference(x, norm_gains, rel_bias, attn_w_in, attn_w_out, hgrn_w_in, hgrn_lb_logits,
              hgrn_out_gain, hgrn_w_out, ffn_w_in, ffn_w_out):
    lb_probs = jax.nn.softmax(hgrn_lb_logits.astype(jnp.float32), axis=0)
    lower_bounds = jnp.cumsum(lb_probs, axis=0) - lb_probs[0]
    h = x
    for i in range(DEPTH):
        gains = norm_gains[i]
        y = rms_norm(h, gains[0])
        if i % N_MIXERS == 0:
            y = dilated_attention_mixer(y, attn_w_in[i // N_MIXERS], attn_w_out[i // N_MIXERS], rel_bias)
        else:
            y = hgrn2_mixer(y, hgrn_w_in[i // N_MIXERS], lower_bounds[i], hgrn_out_gain[i // N_MIXERS], hgrn_w_out[i // N_MIXERS])
        h = h + rms_norm(y, gains[1])
        y = swiglu_ffn(rms_norm(h, gains[2]), ffn_w_in[i], ffn_w_out[i])
        h = h + rms_norm(y, gains[3])
    return h
```

```python
import math
import numpy as np
import concourse.bass as bass
import concourse.mybir as mybir
from concourse.bass_utils import run_bass_kernel_spmd

F32 = mybir.dt.float32
BF16 = mybir.dt.bfloat16
AF = mybir.ActivationFunctionType
ALU = mybir.AluOpType
AX = mybir.AxisListType
P = 128
NEG = -30000.0
EPS = 1e-6
GROUPS = ((128, 1), (512, 4), (2048, 16))
AH = 16
AD = AH * P


class Trk:
    def __init__(self, nc):
        self.nc = nc
        self.eng = {"pe": nc.tensor, "dve": nc.vector, "act": nc.scalar, "pool": nc.gpsimd, "sp": nc.sync}
        self.sems, self.cnt, self.waited, self.res = {}, {}, {}, {}
        self.nins = 0

    def sem(self, k):
        if k not in self.sems:
            self.sems[k] = self.nc.semaphore("s_" + k).__enter__()
        return self.sems[k]

    def op(self, e, fn, R=(), W=(), dma=None, after=()):
        evs = {}
        for k, v in after:
            evs[k] = max(evs.get(k, 0), v)
        for r in R:
            rec = self.res.get(r)
            if rec and rec[0]:
                k, v = rec[0]
                evs[k] = max(evs.get(k, 0), v)
        for w in W:
            rec = self.res.get(w)
            if rec:
                if rec[0]:
                    k, v = rec[0]
                    evs[k] = max(evs.get(k, 0), v)
                for k, v in rec[1].items():
                    evs[k] = max(evs.get(k, 0), v)
        need = [(k, v) for k, v in evs.items() if self.waited.get((e, k), 0) < v]
        for k, v in need[:-1]:
            self.eng[e].wait_ge(self.sem(k), v)
            self.waited[(e, k)] = v
        ins = fn(self.eng[e])
        if need:
            k, v = need[-1]
            ins._wait_ge(self.sem(k), v)
            self.waited[(e, k)] = v
        k, inc = (dma, 16) if dma else (e, 1)
        self.cnt[k] = self.cnt.get(k, 0) + inc
        ins.then_inc(self.sem(k), inc)
        ev = (k, self.cnt[k])
        for r in R:
            self.res.setdefault(r, [None, {}])[1][k] = ev[1]
        for w in W:
            self.res[w] = [ev, {}]
        self.nins += 1
        return ev

    def barrier(self):
        for e in self.eng:
            for k, v in self.cnt.items():
                if self.waited.get((e, k), 0) < v:
                    self.eng[e].wait_ge(self.sem(k), v)
                    self.waited[(e, k)] = v
        self.res = {}


def t5_bucket_np(dist):
    dist = np.maximum(np.asarray(dist, np.int64), 0)
    lr = np.log(np.maximum(dist, 1).astype(np.float32) / np.float32(16)) / np.float32(math.log(2048 / 16))
    large = 16 + (lr.astype(np.float32) * np.float32(16)).astype(np.int32)
    large = np.minimum(large, 31)
    return np.where(dist < 16, dist, large)


def host_constants():
    qi = np.arange(128)[:, None]
    ki = np.arange(256)[None, :]
    rel = qi - ki + 128
    valid = (rel >= 0) & (rel <= 128)
    oh = np.zeros((3, 33, 128 * 256), np.float32)
    for g, (w, d) in enumerate(GROUPS):
        b = t5_bucket_np(rel * d)
        for bb in range(32):
            oh[g, bb] = ((b == bb) & valid).reshape(-1)
        oh[g, 32] = np.where(valid, 0.0, NEG).reshape(-1)
    t = np.arange(128)
    triu = ((t[:, None] <= t[None, :]) & ((t[:, None] // 64) == (t[None, :] // 64))).astype(np.float32)
    return {"c_oh": oh, "c_ident": np.eye(128, dtype=np.float32), "c_triu": triu}


def build(S, D, DFF, TT=256, stop_after=None):
    KC = D // P
    NH = D // P
    NSUB = TT // P
    NT = S // TT
    FT = DFF // P
    nc = bass.Bass("TRN2", target_bir_lowering=False)
    tk = Trk(nc)
    OP = tk.op

    def din(name, shape):
        return nc.dram_tensor(name, list(shape), F32, kind="ExternalInput")

    x = din("x", [S, D]); gains = din("gains", [8, D]); rel_bias = din("rel_bias", [32, 48])
    a_win = din("a_win", [D, 9 * AD]); a_wout = din("a_wout", [AD, D])
    h_win = din("h_win", [D, 4 * D]); lb_log = din("lb_log", [2, D]); h_og = din("h_og", [1, P])
    h_wout = din("h_wout", [D, D])
    f_win = [din("f_win0", [D, 2 * DFF]), din("f_win1", [D, 2 * DFF])]
    f_wout = [din("f_wout0", [DFF, D]), din("f_wout1", [DFF, D])]
    c_oh = din("c_oh", [3, 33, 128 * 256]); c_ident = din("c_ident", [P, P]); c_triu = din("c_triu", [P, P])
    out = nc.dram_tensor("out", [S, D], F32, kind="ExternalOutput")
    h1 = nc.dram_tensor("h1", [S, D], F32); h2 = nc.dram_tensor("h2", [S, D], F32); h3 = nc.dram_tensor("h3", [S, D], F32)
    o_scr = nc.dram_tensor("o_scr", [3, S, AD], F32); l_scr = nc.dram_tensor("l_scr", [3, S, AH], F32)
    b_scr = nc.dram_tensor("b_scr", [3, 48, 128 * 256], F32)

    ctxs = []

    uniq = [0]

    def sb(name, shape, dt):
        uniq[0] += 1
        c = nc.sbuf_tensor(f"{name}_{uniq[0]}", list(shape), dt)
        t = c.__enter__()
        ctxs.append(c)
        return t

    def phase_begin():
        return len(ctxs)

    def phase_end(mark):
        tk.barrier()
        while len(ctxs) > mark:
            ctxs.pop().__exit__(None, None, None)

    ident_f = sb("ident_f", [P, P], F32); ident_b = sb("ident_b", [P, P], BF16)
    triu_f = sb("triu_f", [P, P], F32)
    gainT = sb("gainT", [P, 8, KC], F32)
    gain_b = sb("gain_b", [P, D], F32)
    big = [sb(f"big{i}", [P, max(D, AD)], F32) for i in range(3)]
    ybf = sb("ybf", [P, D], BF16)
    YC = max(KC, 16)
    yT = sb("yT", [P, YC, TT], BF16)
    WBE = 8192
    wbuf = [sb(f"wbuf{i}", [P, WBE], BF16) for i in range(2)]
    st = sb("st", [P, 8], F32)
    psf = [nc.psum_tensor(f"psf{i}", [P, 512], F32).__enter__() for i in range(6)]
    psb = [nc.psum_tensor(f"psb{i}", [P, 1024], BF16).__enter__() for i in range(2)]
    psi = [0, 0]

    def npf():
        psi[0] = (psi[0] + 1) % 6
        return psi[0]

    def npb():
        psi[1] = (psi[1] + 1) % 2
        return psi[1]

    OP("sp", lambda e: e.dma_start(out=ident_f[:], in_=c_ident[:, :]), W=["ident_f"], dma="d_c1")
    OP("sp", lambda e: e.dma_start(out=triu_f[:], in_=c_triu[:, :]), W=["triu_f"], dma="d_c2")
    OP("sp", lambda e: e.dma_start(out=gainT[:], in_=gains.ap().rearrange("r (c p) -> p r c", p=P),
                                   allow_slow_non_contiguous=True), W=["gainT"], dma="d_c3")
    OP("dve", lambda e: e.tensor_copy(out=ident_b[:], in_=ident_f[:]), R=["ident_f"], W=["ident_b"])

    WA, WB = {}, {}
    prep_n = [0]
    cast_i = [0]
    cast_last = [None, None]

    def cast_dma(fn):
        sl = cast_i[0] % 2
        cast_i[0] += 1
        cast_last[sl] = OP("pool", fn, W=[f"castslot{sl}"], dma=f"d_cs{sl}")
        return tuple(ev for ev in cast_last if ev is not None)

    def per_of(KCn, ntiles_call):
        return min(max(1, WBE // (KCn * P)), ntiles_call)

    KPER = WBE // 512

    def prep_fm(wd, base, ntiles, KCn, per):
        gw = per * P
        ng = (ntiles + per - 1) // per
        prep_n[0] += 1
        t = nc.dram_tensor(f"{wd.name}_A{prep_n[0]}", [ng, P, KCn * gw], BF16)
        semk = f"d_cA{prep_n[0]}"
        last = None
        for g_ in range(ng):
            gwn = min(gw, ntiles * P - g_ * gw)
            c0 = base + g_ * gw
            last = cast_dma(lambda e, g_=g_, gwn=gwn, c0=c0: e.dma_start(
                out=t[g_, :, 0:KCn * gwn].rearrange("p (c n) -> p c n", n=gwn),
                in_=wd[0:KCn * P, c0:c0 + gwn].rearrange("(c p) n -> p c n", p=P)))
        WA.setdefault(wd.name, []).append((base, ntiles * P, gw, t, last))

    def prep_tm(wd, base, ncols_total, KCn):
        ncg = ncols_total // 512
        nkg = (KCn + KPER - 1) // KPER
        prep_n[0] += 1
        t = nc.dram_tensor(f"{wd.name}_B{prep_n[0]}", [ncg, nkg, P, KPER * 512], BF16)
        semk = f"d_cB{prep_n[0]}"
        last = None
        for cg in range(ncg):
            for kg in range(nkg):
                kn = min(KPER, KCn - kg * KPER)
                last = cast_dma(lambda e, cg=cg, kg=kg, kn=kn: e.dma_start(
                    out=t[cg, kg, :, 0:kn * 512].rearrange("p (c n) -> p c n", n=512),
                    in_=wd[kg * KPER * P:(kg * KPER + kn) * P, base + cg * 512:base + (cg + 1) * 512].rearrange("(c p) n -> p c n", p=P)))
        WB.setdefault(wd.name, []).append((base, ncols_total, t, last))

    for g_ in range(3):
        prep_fm(a_win, g_ * 3 * AD, 2 * AH, KC, per_of(KC, AH))
        prep_tm(a_win, g_ * 3 * AD + 2 * AD, AD, KC)
    prep_tm(a_wout, 0, D, AH)
    prep_fm(f_win[0], 0, 2 * FT, KC, per_of(KC, FT)); prep_tm(f_wout[0], 0, D, FT)
    prep_fm(h_win, 0, 2 * NH, KC, 1); prep_tm(h_win, 2 * D, 2 * D, KC); prep_tm(h_wout, 0, D, KC)
    prep_fm(f_win[1], 0, 2 * FT, KC, per_of(KC, FT)); prep_tm(f_wout[1], 0, D, FT)

    wq = [0]

    def load_fm(wd, n0, tn, KCn):
        for (base, ncl, gw, t, res) in WA[wd.name]:
            if base <= n0 < base + ncl:
                break
        else:
            raise KeyError((wd.name, n0))
        assert (n0 - base) % gw == 0 and tn * P <= gw
        gi = (n0 - base) // gw
        i = wq[0] % 2
        wq[0] += 1
        ne = KCn * tn * P
        OP("sp", lambda e: e.dma_start(out=wbuf[i][:, 0:ne], in_=t[gi, :, 0:ne]), after=res, W=[f"wbuf{i}"], dma=f"d_w{i}")
        return i, wbuf[i][:, 0:ne].rearrange("p (c n) -> p c n", n=tn * P)

    def load_tm(wd, k0, kn, n0):
        for (base, ncl, t, res) in WB[wd.name]:
            if base <= n0 < base + ncl:
                break
        else:
            raise KeyError((wd.name, n0))
        assert (n0 - base) % 512 == 0 and k0 % KPER == 0
        i = wq[0] % 2
        wq[0] += 1
        ne = kn * 512
        OP("sp", lambda e: e.dma_start(out=wbuf[i][:, 0:ne], in_=t[(n0 - base) // 512, k0 // KPER, :, 0:ne]), after=res, W=[f"wbuf{i}"], dma=f"d_w{i}")
        return i, wbuf[i][:, 0:ne].rearrange("p (c n) -> p c n", n=512)

    def load_gain_b(row):
        OP("sp", lambda e: e.dma_start(out=gain_b[:], in_=gains[row:row + 1, :].partition_broadcast(P)),
           W=["gain_b"], dma="d_gb")

    ev_flip = [0]

    def evac(out_ap, in_ap, R, W, scale=None):
        ev_flip[0] ^= 1
        if ev_flip[0]:
            if scale is None:
                OP("act", lambda e: e.copy(out=out_ap, in_=in_ap), R=R, W=W)
            else:
                OP("act", lambda e: e.activation(out=out_ap, in_=in_ap, func=AF.Copy, scale=scale), R=R, W=W)
        else:
            if scale is None:
                OP("dve", lambda e: e.tensor_copy(out=out_ap, in_=in_ap), R=R, W=W)
            else:
                OP("dve", lambda e: e.tensor_scalar(out=out_ap, in0=in_ap, scalar1=scale, scalar2=None, op0=ALU.mult), R=R, W=W)

    def rstd_of(src_ap, src_res, n_feat, col=0):
        OP("act", lambda e: e.activation(out=ybf[:, 0:n_feat], in_=src_ap, func=AF.Square, accum_out=st[:, 6:7]),
           R=src_res, W=["ybf", "st6"])
        OP("dve", lambda e: e.tensor_scalar(out=st[:, 7:8], in0=st[:, 6:7], scalar1=1.0 / n_feat, scalar2=EPS,
                                            op0=ALU.mult, op1=ALU.add), R=["st6"], W=["st7"])
        OP("act", lambda e: e.activation(out=st[:, 7:8], in_=st[:, 7:8], func=AF.Sqrt), R=["st7"], W=["st7"])
        OP("dve", lambda e: e.reciprocal(out=st[:, col:col + 1], in_=st[:, 7:8]), R=["st7"], W=[f"st{col}"])

    def prenorm_T(rows_ap, sub, grow, dstT=None, KCn=None):
        OP("sp", lambda e: e.dma_start(out=big[0][:, 0:D], in_=rows_ap), W=["big0"], dma="d_x")
        rstd_of(big[0][:, 0:D], ["big0"], D, col=0)
        OP("dve", lambda e: e.tensor_scalar(out=ybf[:], in0=big[0][:, 0:D], scalar1=st[:, 0:1], scalar2=None, op0=ALU.mult),
           R=["big0", "st0"], W=["ybf"])
        transpose_into(ybf, "ybf", KC, sub, grow)

    def transpose_into(src, src_res, kcn, sub, grow=None):
        for k0 in range(0, kcn, 8):
            kn = min(8, kcn - k0)
            b = npb()
            for j in range(kn):
                OP("pe", lambda e, j=j: e.transpose(out=psb[b][:, j * P:(j + 1) * P], in_=src[:, (k0 + j) * P:(k0 + j + 1) * P],
                                                    identity=ident_b[:]), R=[src_res, "ident_b"], W=[f"psb{b}"])
            for j in range(kn):
                kc = k0 + j
                dst = yT[:, kc, sub * P:(sub + 1) * P]
                if grow is None:
                    evac(dst, psb[b][:, j * P:(j + 1) * P], [f"psb{b}"], ["yT"])
                else:
                    evac(dst, psb[b][:, j * P:(j + 1) * P], [f"psb{b}", "gainT"], ["yT"], scale=gainT[:, grow, kc:kc + 1])

    def dense_fm(wd, col0, ntiles, KCn, rhs_fn, rhs_res, T, evac_fn, pair_offset=None):
        per = per_of(KCn, ntiles)
        for t0 in range(0, ntiles, per):
            tn = min(per, ntiles - t0)
            i, wv = load_fm(wd, col0 + t0 * P, tn, KCn)
            for tl in range(tn):
                b = npf()
                for kc in range(KCn):
                    OP("pe", lambda e, kc=kc, tl=tl: e.matmul(psf[b][:, 0:T], lhsT=wv[:, kc, tl * P:(tl + 1) * P], rhs=rhs_fn(kc),
                                                              start=(kc == 0), stop=(kc == KCn - 1)),
                       R=[f"wbuf{i}"] + rhs_res, W=[f"psf{b}"])
                evac_fn(t0 + tl, b)

    def dense_tm(wd, col0, ncols, KCn, lhsT_fn, lhs_res, nsub, evac_fn):
        assert ncols == 512
        kper = KPER
        bs = [npf() for _ in range(nsub)]
        for k0 in range(0, KCn, kper):
            kn = min(kper, KCn - k0)
            i, wv = load_tm(wd, k0, kn, col0)
            for s_ in range(nsub):
                for kl in range(kn):
                    kc = k0 + kl
                    OP("pe", lambda e, kc=kc, kl=kl, s_=s_: e.matmul(psf[bs[s_]][:, 0:ncols], lhsT=lhsT_fn(kc, s_), rhs=wv[:, kl, :],
                                                                      start=(kc == 0), stop=(kc == KCn - 1)),
                       R=[f"wbuf{i}"] + lhs_res, W=[f"psf{bs[s_]}"])
        for s_ in range(nsub):
            evac_fn(s_, bs[s_])

    def postnorm_residual(sub, ybuf, ybuf_res, res_rows_ap, dst_rows_ap, n_feat=None):
        rstd_of(ybuf[:, 0:D], [ybuf_res], D, col=1)
        OP("dve", lambda e: e.scalar_tensor_tensor(out=ybuf[:, 0:D], in0=ybuf[:, 0:D], scalar=st[:, 1:2], in1=gain_b[:],
                                                   op0=ALU.mult, op1=ALU.mult), R=[ybuf_res, "st1", "gain_b"], W=[ybuf_res])
        OP("sp", lambda e: e.dma_start(out=big[0][:, 0:D], in_=res_rows_ap), W=["big0"], dma="d_x")
        OP("dve", lambda e: e.tensor_tensor(out=ybuf[:, 0:D], in0=ybuf[:, 0:D], in1=big[0][:, 0:D], op=ALU.add), R=[ybuf_res, "big0"], W=[ybuf_res])
        OP("sp", lambda e: e.dma_start(out=dst_rows_ap, in_=ybuf[:, 0:D]), R=[ybuf_res], dma="d_st_" + ybuf_res)

    def attention_phase():
        mark = phase_begin()
        rb33 = sb("rb33", [33, 48], F32)
        ohs = [sb(f"ohs{i}", [33, 512], F32) for i in range(2)]
        bst = [sb(f"bst{i}", [48, 512], F32) for i in range(2)]
        bias_sb = sb("bias_sb", [P, AH, 256], F32)
        QT = sb("QT", [P, AH, TT], BF16)
        KT = sb("KT", [P, AH, TT + P], BF16)
        V = sb("V", [P, NSUB + 1, AD], BF16)
        Ssb = [sb(f"Ssb{i}", [P, 256], F32) for i in range(2)]
        Pb = [sb(f"Pb{i}", [P, 256], BF16) for i in range(2)]
        PT = [sb(f"PT{i}", [P, 2, P], BF16) for i in range(2)]
        hst = sb("hst", [P, AH, 4], F32)
        Lsb = sb("Lsb", [P, AH], F32)
        OP("sp", lambda e: e.dma_start(out=rb33[0:32, :], in_=rel_bias[:, :]), W=["rb33a"], dma="d_rb")
        OP("dve", lambda e: e.memset(rb33[32:33, :], 1.0), W=["rb33b"])
        xr = [x.ap().rearrange("(m d) f -> d m f", d=d) for (_, d) in GROUPS]
        for g, (w_, d) in enumerate(GROUPS):
            Sd = S // d
            osr = o_scr[g].rearrange("(m d) f -> d m f", d=d)
            lsr = l_scr[g].rearrange("(m d) f -> d m f", d=d)
            for c in range(64):
                i = c % 2
                OP("sp", lambda e: e.dma_start(out=ohs[i][:], in_=c_oh[g, :, c * 512:(c + 1) * 512]), W=[f"ohs{i}"], dma=f"d_oh{i}")
                b = npf()
                OP("pe", lambda e: e.matmul(psf[b][0:48, 0:512], lhsT=rb33[:, :], rhs=ohs[i][:, :], start=True, stop=True),
                   R=["rb33a", "rb33b", f"ohs{i}"], W=[f"psf{b}"])
                evac(bst[i][:], psf[b][0:48, 0:512], [f"psf{b}"], [f"bst{i}"])
                OP("sp", lambda e: e.dma_start(out=b_scr[g, :, c * 512:(c + 1) * 512], in_=bst[i][:]), R=[f"bst{i}"], W=["b_scr"], dma=f"d_bs{i}")
            OP("sp", lambda e: e.dma_start(out=bias_sb[:], in_=b_scr[g, g * AH:(g + 1) * AH, :].rearrange("h (q k) -> q h k", k=256)),
               R=["b_scr"], W=["bias_sb"], dma="d_bias")
            for j in range(NT):
                r = (j * TT) // Sd
                m0 = (j * TT) % Sd
                first = (m0 == 0)
                for sub in range(NSUB):
                    prenorm_T(xr[g][r, m0 + sub * P:m0 + (sub + 1) * P, :], sub, 0)
                cb = g * 3 * AD

                def ev_q(t, b):
                    evac(QT[:, t, :], psf[b][:, 0:TT], [f"psf{b}"], ["QT"], scale=float(P) ** -0.5)

                def ev_k(t, b):
                    evac(KT[:, t, P:P + TT], psf[b][:, 0:TT], [f"psf{b}"], ["KT"])
                dense_fm(a_win, cb, AH, KC, lambda kc: yT[:, kc, :], ["yT"], TT, ev_q)
                dense_fm(a_win, cb + AD, AH, KC, lambda kc: yT[:, kc, :], ["yT"], TT, ev_k)
                for c in range(AD // 512):
                    def ev_v(s_, b, c=c):
                        evac(V[:, 1 + s_, c * 512:(c + 1) * 512], psf[b][:, 0:512], [f"psf{b}"], ["V"])
                    dense_tm(a_win, cb + 2 * AD + c * 512, 512, KC, lambda kc, s_: yT[:, kc, s_ * P:(s_ + 1) * P], ["yT"], NSUB, ev_v)
                for blk in range(NSUB):
                    if first and blk == 0:
                        nk, ko = P, P
                    else:
                        nk, ko = 2 * P, 0
                    for h in range(AH):
                        u = h % 2
                        b = npf()
                        OP("pe", lambda e: e.matmul(psf[b][:, 0:nk], lhsT=QT[:, h, blk * P:(blk + 1) * P],
                                                    rhs=KT[:, h, blk * P + ko:blk * P + ko + nk], start=True, stop=True),
                           R=["QT", "KT"], W=[f"psf{b}"])
                        OP("dve", lambda e: e.tensor_tensor(out=Ssb[u][:, 0:nk], in0=psf[b][:, 0:nk], in1=bias_sb[:, h, ko:ko + nk], op=ALU.add),
                           R=[f"psf{b}", "bias_sb"], W=[f"Ssb{u}"])
                        OP("dve", lambda e: e.tensor_reduce(out=hst[:, h, 0:1], in_=Ssb[u][:, 0:nk], axis=AX.X, op=ALU.max),
                           R=[f"Ssb{u}"], W=[f"hs0_{h}"])
                        OP("dve", lambda e: e.tensor_scalar(out=hst[:, h, 1:2], in0=hst[:, h, 0:1], scalar1=-1.0, scalar2=None, op0=ALU.mult),
                           R=[f"hs0_{h}"], W=[f"hs1_{h}"])
                        OP("act", lambda e: e.activation(out=Pb[u][:, 0:nk], in_=Ssb[u][:, 0:nk], func=AF.Exp, bias=hst[:, h, 1:2],
                                                         accum_out=hst[:, h, 2:3]), R=[f"Ssb{u}", f"hs1_{h}"], W=[f"Pb{u}", f"hs2_{h}"])
                        nkb = nk // P
                        bb = npb()
                        for kb in range(nkb):
                            OP("pe", lambda e, kb=kb: e.transpose(out=psb[bb][:, kb * P:(kb + 1) * P], in_=Pb[u][:, kb * P:(kb + 1) * P],
                                                                  identity=ident_b[:]), R=[f"Pb{u}", "ident_b"], W=[f"psb{bb}"])
                        OP("dve", lambda e: e.tensor_copy(out=PT[u][:, 0:nkb, :], in_=psb[bb][:, 0:nkb * P].rearrange("p (a b) -> p a b", b=P)),
                           R=[f"psb{bb}"], W=[f"PT{u}"])
                        b2 = npf()
                        for kb in range(nkb):
                            OP("pe", lambda e, kb=kb: e.matmul(psf[b2][:, 0:P], lhsT=PT[u][:, kb, :],
                                                                rhs=V[:, blk + ko // P + kb, h * P:(h + 1) * P], start=(kb == 0), stop=(kb == nkb - 1)),
                               R=[f"PT{u}", "V"], W=[f"psf{b2}"])
                        OP("dve", lambda e: e.reciprocal(out=hst[:, h, 3:4], in_=hst[:, h, 2:3]), R=[f"hs2_{h}"], W=[f"hs3_{h}"])
                        OP("act", lambda e: e.activation(out=big[1][:, h * P:(h + 1) * P], in_=psf[b2][:, 0:P], func=AF.Copy, scale=hst[:, h, 3:4]),
                           R=[f"psf{b2}", f"hs3_{h}"], W=["big1"])
                        OP("act", lambda e: e.activation(out=Lsb[:, h:h + 1], in_=hst[:, h, 2:3], func=AF.Ln), R=[f"hs2_{h}"], W=["Lsb"])
                        OP("dve", lambda e: e.tensor_tensor(out=Lsb[:, h:h + 1], in0=Lsb[:, h:h + 1], in1=hst[:, h, 0:1], op=ALU.add),
                           R=["Lsb", f"hs0_{h}"], W=["Lsb"])
                    rows = slice(m0 + blk * P, m0 + (blk + 1) * P)
                    OP("sp", lambda e: e.dma_start(out=osr[r, rows, :], in_=big[1][:, 0:AD]), R=["big1"], W=["o_scr"], dma="d_os")
                    OP("sp", lambda e: e.dma_start(out=lsr[r, rows, :], in_=Lsb[:]), R=["Lsb"], W=["l_scr"], dma="d_ls")
                OP("act", lambda e: e.copy(out=KT[:, :, 0:P], in_=KT[:, :, TT:TT + P]), R=["KT"], W=["KT"])
                OP("act", lambda e: e.copy(out=V[:, 0, :], in_=V[:, NSUB, :]), R=["V"], W=["V"])
        phase_end(mark)

    def merge_phase():
        mark = phase_begin()
        Lg = sb("Lg", [P, 3, AH], F32)
        al = sb("al", [P, 3, AH], F32)
        mx = sb("mx", [P, AH], F32)
        mrg = sb("mrg", [P, AD], F32)
        tmp = sb("tmp", [P, AD], F32)
        mbf = sb("mbf", [P, AD], BF16)
        load_gain_b(1)
        og0 = sb("og0", [P, AD], F32); og1 = sb("og1", [P, AD], F32); og2 = sb("og2", [P, AD], F32)
        Og = [og0[:], og1[:], og2[:]]
        for j in range(NT):
            for sub in range(NSUB):
                t0 = j * TT + sub * P
                for g in range(3):
                    OP("sp", lambda e, g=g: e.dma_start(out=Og[g], in_=o_scr[g, t0:t0 + P, :]), R=["o_scr"], W=[f"Og{g}"], dma=f"d_og{g}")
                OP("sp", lambda e: e.dma_start(out=Lg[:], in_=l_scr[:, t0:t0 + P, :].rearrange("g t h -> t g h")), R=["l_scr"], W=["Lg"], dma="d_lg")
                OP("dve", lambda e: e.tensor_tensor(out=mx[:], in0=Lg[:, 0, :], in1=Lg[:, 1, :], op=ALU.max), R=["Lg"], W=["mx"])
                OP("dve", lambda e: e.tensor_tensor(out=mx[:], in0=mx[:], in1=Lg[:, 2, :], op=ALU.max), R=["Lg", "mx"], W=["mx"])
                for g in range(3):
                    OP("dve", lambda e, g=g: e.tensor_tensor(out=al[:, g, :], in0=Lg[:, g, :], in1=mx[:], op=ALU.subtract), R=["Lg", "mx"], W=["al"])
                OP("act", lambda e: e.activation(out=al[:], in_=al[:], func=AF.Exp), R=["al"], W=["al"])
                OP("dve", lambda e: e.tensor_tensor(out=mx[:], in0=al[:, 0, :], in1=al[:, 1, :], op=ALU.add), R=["al"], W=["mx"])
                OP("dve", lambda e: e.tensor_tensor(out=mx[:], in0=mx[:], in1=al[:, 2, :], op=ALU.add), R=["al", "mx"], W=["mx"])
                OP("dve", lambda e: e.reciprocal(out=mx[:], in_=mx[:]), R=["mx"], W=["mx"])
                for g in range(3):
                    OP("dve", lambda e, g=g: e.tensor_tensor(out=al[:, g, :], in0=al[:, g, :], in1=mx[:], op=ALU.mult), R=["al", "mx"], W=["al"])

                def v3(ap):
                    return ap.rearrange("p (h d) -> p h d", d=P)

                def bc(g):
                    return al[:, g, :].unsqueeze(2).to_broadcast([P, AH, P])
                OP("dve", lambda e: e.tensor_tensor(out=v3(mrg[:]), in0=v3(Og[0]), in1=bc(0), op=ALU.mult), R=["Og0", "al"], W=["mrg"])
                OP("dve", lambda e: e.tensor_tensor(out=v3(tmp[:]), in0=v3(Og[1]), in1=bc(1), op=ALU.mult), R=["Og1", "al"], W=["tmp"])
                OP("dve", lambda e: e.tensor_tensor(out=mrg[:], in0=mrg[:], in1=tmp[:], op=ALU.add), R=["mrg", "tmp"], W=["mrg"])
                OP("dve", lambda e: e.tensor_tensor(out=v3(tmp[:]), in0=v3(Og[2]), in1=bc(2), op=ALU.mult), R=["Og2", "al"], W=["tmp"])
                OP("dve", lambda e: e.tensor_tensor(out=mbf[:], in0=mrg[:], in1=tmp[:], op=ALU.add), R=["mrg", "tmp"], W=["mbf"])
                transpose_into(mbf, "mbf", AH, sub)
            yb = [big[1], big[2]]
            ybr = ["big1", "big2"]
            for c in range(D // 512):
                def ev_y(s_, b, c=c):
                    evac(yb[s_][:, c * 512:(c + 1) * 512], psf[b][:, 0:512], [f"psf{b}"], [ybr[s_]])
                dense_tm(a_wout, c * 512, 512, AH, lambda kc, s_: yT[:, kc, s_ * P:(s_ + 1) * P], ["yT"], NSUB, ev_y)
            for sub in range(NSUB):
                t0 = j * TT + sub * P
                postnorm_residual(sub, yb[sub], ybr[sub], x[t0:t0 + P, :], h1[t0:t0 + P, :])
        phase_end(mark)

    def ffn_phase(l, hs, hd):
        mark = phase_begin()
        HT = sb("HT", [P, FT, TT], BF16)
        sg = [sb(f"sg{i}", [P, TT], F32) for i in range(2)]
        load_gain_b(4 * l + 3)
        for j in range(NT):
            for sub in range(NSUB):
                t0 = j * TT + sub * P
                prenorm_T(hs[t0:t0 + P, :], sub, 4 * l + 2)
            per = per_of(KC, FT)
            for t0_ in range(0, FT, per):
                tn = min(per, FT - t0_)
                ig, wg = load_fm(f_win[l], t0_ * P, tn, KC)
                iu, wu = load_fm(f_win[l], DFF + t0_ * P, tn, KC)
                for tl in range(tn):
                    t = t0_ + tl
                    bg = npf(); bu = npf()
                    for kc in range(KC):
                        OP("pe", lambda e, kc=kc: e.matmul(psf[bg][:, 0:TT], lhsT=wg[:, kc, tl * P:(tl + 1) * P], rhs=yT[:, kc, :],
                                                           start=(kc == 0), stop=(kc == KC - 1)), R=[f"wbuf{ig}", "yT"], W=[f"psf{bg}"])
                    for kc in range(KC):
                        OP("pe", lambda e, kc=kc: e.matmul(psf[bu][:, 0:TT], lhsT=wu[:, kc, tl * P:(tl + 1) * P], rhs=yT[:, kc, :],
                                                           start=(kc == 0), stop=(kc == KC - 1)), R=[f"wbuf{iu}", "yT"], W=[f"psf{bu}"])
                    u = t % 2
                    OP("act", lambda e: e.activation(out=sg[u][:], in_=psf[bg][:, 0:TT], func=AF.Silu), R=[f"psf{bg}"], W=[f"sg{u}"])
                    OP("dve", lambda e: e.tensor_tensor(out=HT[:, t, :], in0=sg[u][:], in1=psf[bu][:, 0:TT], op=ALU.mult),
                       R=[f"sg{u}", f"psf{bu}"], W=["HT"])
            yb = [big[1], big[2]]
            ybr = ["big1", "big2"]
            for c in range(D // 512):
                def ev_y(s_, b, c=c):
                    evac(yb[s_][:, c * 512:(c + 1) * 512], psf[b][:, 0:512], [f"psf{b}"], [ybr[s_]])
                dense_tm(f_wout[l], c * 512, 512, FT, lambda kc, s_: HT[:, kc, s_ * P:(s_ + 1) * P], ["HT"], NSUB, ev_y)
            for sub in range(NSUB):
                t0 = j * TT + sub * P
                postnorm_residual(sub, yb[sub], ybr[sub], hs[t0:t0 + P, :], hd[t0:t0 + P, :])
        phase_end(mark)

    def hgrn_phase(hs, hd):
        mark = phase_begin()
        NCH = TT // 64
        lbt = sb("lbt", [P, 2, NH], F32)
        lbT = sb("lbT", [P, NH], F32); omlT = sb("omlT", [P, NH], F32)
        og_b = sb("og_b", [P, P], F32)
        ones = sb("ones", [P, 64], F32)
        S32 = sb("S32", [P, NH, P], F32); Sbf = sb("Sbf", [P, NH, P], BF16)
        V = sb("Vh", [P, NSUB, D], BF16)
        Ob = sb("Ob", [P, NSUB, D], BF16)
        ssq = sb("ssq", [P, NSUB, NH], F32)
        f32t = {n: sb("t_" + n, [P, TT], F32) for n in ("qs", "fg", "lf", "kT", "bT", "bm", "bl", "E1", "E2", "E3", "E4")}
        dec = sb("dec", [P, NCH], F32)
        qd = sb("qd", [P, TT], BF16); kd = sb("kd", [P, TT], BF16); klT = sb("klT", [P, TT], BF16)
        qbz = sb("qbz", [P, NCH, P], BF16)
        kl = sb("kl", [P, NSUB, P], BF16)
        ATm = sb("ATm", [P, P], BF16)
        junk = sb("junk", [P, P], BF16)
        Gs = sb("Gs", [P, 512], F32)
        tm3 = sb("tm3", [P, 512], F32)
        OP("sp", lambda e: e.dma_start(out=lbt[:], in_=lb_log.ap().rearrange("r (h p) -> p r h", p=P), allow_slow_non_contiguous=True),
           W=["lbt"], dma="d_lb")
        OP("sp", lambda e: e.dma_start(out=og_b[:], in_=h_og[0:1, :].partition_broadcast(P)), W=["og_b"], dma="d_og")
        OP("dve", lambda e: e.tensor_tensor(out=lbT[:], in0=lbt[:, 1, :], in1=lbt[:, 0, :], op=ALU.subtract), R=["lbt"], W=["lbT"])
        OP("act", lambda e: e.activation(out=lbT[:], in_=lbT[:], func=AF.Sigmoid), R=["lbT"], W=["lbT"])
        OP("dve", lambda e: e.tensor_scalar(out=omlT[:], in0=lbT[:], scalar1=-1.0, scalar2=1.0, op0=ALU.mult, op1=ALU.add), R=["lbT"], W=["omlT"])
        OP("dve", lambda e: e.memset(ones[:], 1.0), W=["ones"])
        OP("dve", lambda e: e.memset(S32[:], 0.0), W=["S32"])
        OP("dve", lambda e: e.memset(Sbf[:], 0.0), W=["Sbf"])
        OP("dve", lambda e: e.memset(qbz[:], 0.0), W=["qbz"])
        load_gain_b(5)
        T = f32t
        sc = float(P) ** -0.5

        def c3(ap):
            return ap.rearrange("p (c t) -> p c t", t=64)
        for j in range(NT):
            for sub in range(NSUB):
                t0 = j * TT + sub * P
                prenorm_T(hs[t0:t0 + P, :], sub, 4)
            for c in range(D // 512):
                def ev_v(s_, b, c=c):
                    evac(V[:, s_, c * 512:(c + 1) * 512], psf[b][:, 0:512], [f"psf{b}"], ["Vh"])
                dense_tm(h_win, 2 * D + c * 512, 512, KC, lambda kc, s_: yT[:, kc, s_ * P:(s_ + 1) * P], ["yT"], NSUB, ev_v)
            for h in range(NH):
                def ev_q(t, b):
                    OP("act", lambda e: e.activation(out=T["qs"][:], in_=psf[b][:, 0:TT], func=AF.Silu), R=[f"psf{b}"], W=["qs"])

                def ev_f(t, b):
                    OP("act", lambda e: e.activation(out=T["fg"][:], in_=psf[b][:, 0:TT], func=AF.Sigmoid), R=[f"psf{b}"], W=["fg"])
                dense_fm(h_win, h * P, 1, KC, lambda kc: yT[:, kc, :], ["yT"], TT, ev_q)
                dense_fm(h_win, D + h * P, 1, KC, lambda kc: yT[:, kc, :], ["yT"], TT, ev_f)
                OP("dve", lambda e: e.tensor_scalar(out=T["fg"][:], in0=T["fg"][:], scalar1=omlT[:, h:h + 1], scalar2=lbT[:, h:h + 1],
                                                    op0=ALU.mult, op1=ALU.add), R=["fg", "omlT", "lbT"], W=["fg"])
                OP("act", lambda e: e.activation(out=T["lf"][:], in_=T["fg"][:], func=AF.Ln), R=["fg"], W=["lf"])
                OP("dve", lambda e: e.tensor_scalar(out=T["kT"][:], in0=T["fg"][:], scalar1=-1.0, scalar2=1.0, op0=ALU.mult, op1=ALU.add),
                   R=["fg"], W=["kT"])
                for c in range(NCH):
                    OP("dve", lambda e, c=c: e.tensor_tensor_scan(out=T["bT"][:, c * 64:(c + 1) * 64], data0=ones[:], data1=T["lf"][:, c * 64:(c + 1) * 64],
                                                                   initial=0.0, op0=ALU.mult, op1=ALU.add), R=["lf", "ones"], W=["bT"])
                OP("dve", lambda e: e.tensor_tensor(out=c3(T["bm"][:]), in0=c3(T["bT"][:]), in1=c3(T["bT"][:])[:, :, 31:32].to_broadcast([P, NCH, 64]),
                                                    op=ALU.subtract), R=["bT"], W=["bm"])
                OP("dve", lambda e: e.tensor_tensor(out=c3(T["bl"][:]), in0=c3(T["bT"][:])[:, :, 63:64].to_broadcast([P, NCH, 64]), in1=c3(T["bT"][:]),
                                                    op=ALU.subtract), R=["bT"], W=["bl"])
                OP("act", lambda e: e.activation(out=T["E1"][:], in_=T["bm"][:], func=AF.Exp), R=["bm"], W=["E1"])
                OP("act", lambda e: e.activation(out=T["E2"][:], in_=T["bm"][:], func=AF.Exp, scale=-1.0), R=["bm"], W=["E2"])
                OP("act", lambda e: e.activation(out=T["E3"][:], in_=T["bT"][:], func=AF.Exp), R=["bT"], W=["E3"])
                OP("act", lambda e: e.activation(out=T["E4"][:], in_=T["bl"][:], func=AF.Exp), R=["bl"], W=["E4"])
                OP("act", lambda e: e.activation(out=dec[:].unsqueeze(2), in_=c3(T["bT"][:])[:, :, 63:64], func=AF.Exp), R=["bT"], W=["dec"])
                OP("dve", lambda e: e.scalar_tensor_tensor(out=qd[:], in0=T["qs"][:], scalar=sc, in1=T["E1"][:], op0=ALU.mult, op1=ALU.mult),
                   R=["qs", "E1"], W=["qd"])
                for c in range(NCH):
                    hf = c % 2
                    OP("dve", lambda e, c=c, hf=hf: e.scalar_tensor_tensor(out=qbz[:, c, hf * 64:(hf + 1) * 64], in0=T["qs"][:, c * 64:(c + 1) * 64], scalar=sc,
                                                                            in1=T["E3"][:, c * 64:(c + 1) * 64], op0=ALU.mult, op1=ALU.mult),
                       R=["qs", "E3"], W=["qbz"])
                OP("dve", lambda e: e.tensor_tensor(out=kd[:], in0=T["kT"][:], in1=T["E2"][:], op=ALU.mult), R=["kT", "E2"], W=["kd"])
                OP("dve", lambda e: e.tensor_tensor(out=klT[:], in0=T["kT"][:], in1=T["E4"][:], op=ALU.mult), R=["kT", "E4"], W=["klT"])
                bb = npb()
                for s_ in range(NSUB):
                    OP("pe", lambda e, s_=s_: e.transpose(out=psb[bb][:, s_ * P:(s_ + 1) * P], in_=klT[:, s_ * P:(s_ + 1) * P], identity=ident_b[:]),
                       R=["klT", "ident_b"], W=[f"psb{bb}"])
                OP("dve", lambda e: e.tensor_copy(out=kl[:], in_=psb[bb][:, 0:NSUB * P].rearrange("p (a b) -> p a b", b=P)), R=[f"psb{bb}"], W=["kl"])
                for s_ in range(NSUB):
                    ba = npf()
                    OP("pe", lambda e: e.matmul(psf[ba][:, 0:P], lhsT=kd[:, s_ * P:(s_ + 1) * P], rhs=qd[:, s_ * P:(s_ + 1) * P], start=True, stop=True),
                       R=["kd", "qd"], W=[f"psf{ba}"])
                    OP("dve", lambda e: e.tensor_tensor(out=ATm[:], in0=psf[ba][:, 0:P], in1=triu_f[:], op=ALU.mult), R=[f"psf{ba}", "triu_f"], W=["ATm"])
                    bo = npf()
                    OP("pe", lambda e: e.matmul(psf[bo][:, 0:P], lhsT=ATm[:], rhs=V[:, s_, h * P:(h + 1) * P], start=True, stop=False),
                       R=["ATm", "Vh"], W=[f"psf{bo}"])
                    for hf in range(2):
                        c = s_ * 2 + hf
                        OP("pe", lambda e, c=c, hf=hf: e.matmul(psf[bo][:, 0:P], lhsT=qbz[:, c, :], rhs=Sbf[:, h, :], start=False, stop=(hf == 1)),
                           R=["qbz", f"Sbf{h}"], W=[f"psf{bo}"])
                        bn = npf()
                        OP("pe", lambda e, hf=hf: e.matmul(psf[bn][:, 0:P], lhsT=kl[hf * 64:(hf + 1) * 64, s_, :], rhs=V[hf * 64:(hf + 1) * 64, s_, h * P:(h + 1) * P],
                                                           start=True, stop=True), R=["kl", "Vh"], W=[f"psf{bn}"])
                        OP("dve", lambda e, c=c: e.scalar_tensor_tensor(out=S32[:, h, :], in0=S32[:, h, :], scalar=dec[:, c:c + 1], in1=psf[bn][:, 0:P],
                                                                         op0=ALU.mult, op1=ALU.add), R=[f"S32{h}", "dec", f"psf{bn}"], W=[f"S32{h}"])
                        OP("act", lambda e: e.copy(out=Sbf[:, h, :], in_=S32[:, h, :]), R=[f"S32{h}"], W=[f"Sbf{h}"])
                    OP("act", lambda e: e.activation(out=junk[:], in_=psf[bo][:, 0:P], func=AF.Square, accum_out=ssq[:, s_, h:h + 1]),
                       R=[f"psf{bo}"], W=["junk", "ssq"])
                    OP("dve", lambda e: e.tensor_copy(out=Ob[:, s_, h * P:(h + 1) * P], in_=psf[bo][:, 0:P]), R=[f"psf{bo}"], W=["Ob"])
            OP("dve", lambda e: e.tensor_scalar(out=ssq[:], in0=ssq[:], scalar1=1.0 / P, scalar2=EPS, op0=ALU.mult, op1=ALU.add), R=["ssq"], W=["ssq"])
            OP("act", lambda e: e.activation(out=ssq[:], in_=ssq[:], func=AF.Sqrt), R=["ssq"], W=["ssq"])
            OP("dve", lambda e: e.reciprocal(out=ssq[:], in_=ssq[:]), R=["ssq"], W=["ssq"])
            for c in range(D // 512):
                def ev_g(s_, b, c=c):
                    hh = 512 // P
                    OP("act", lambda e: e.activation(out=Gs[:], in_=psf[b][:, 0:512], func=AF.Silu), R=[f"psf{b}"], W=["Gs"])
                    ov = Ob[:, s_, c * 512:(c + 1) * 512].rearrange("p (h d) -> p h d", d=P)
                    t3 = tm3[:].rearrange("p (h d) -> p h d", d=P)
                    OP("dve", lambda e: e.tensor_tensor(out=t3, in0=ov, in1=ssq[:, s_, c * hh:(c + 1) * hh].unsqueeze(2).to_broadcast([P, hh, P]), op=ALU.mult),
                       R=["Ob", "ssq"], W=["tm3"])
                    OP("dve", lambda e: e.tensor_tensor(out=t3, in0=t3, in1=og_b[:].unsqueeze(1).to_broadcast([P, hh, P]), op=ALU.mult),
                       R=["tm3", "og_b"], W=["tm3"])
                    OP("dve", lambda e: e.tensor_tensor(out=Ob[:, s_, c * 512:(c + 1) * 512], in0=tm3[:], in1=Gs[:], op=ALU.mult), R=["tm3", "Gs"], W=["Ob"])
                dense_tm(h_win, 3 * D + c * 512, 512, KC, lambda kc, s_: yT[:, kc, s_ * P:(s_ + 1) * P], ["yT"], NSUB, ev_g)
            for sub in range(NSUB):
                transpose_into(Ob[:, sub, :], "Ob", KC, sub)
            yb = [big[1], big[2]]
            ybr = ["big1", "big2"]
            for c in range(D // 512):
                def ev_y(s_, b, c=c):
                    evac(yb[s_][:, c * 512:(c + 1) * 512], psf[b][:, 0:512], [f"psf{b}"], [ybr[s_]])
                dense_tm(h_wout, c * 512, 512, KC, lambda kc, s_: yT[:, kc, s_ * P:(s_ + 1) * P], ["yT"], NSUB, ev_y)
            for sub in range(NSUB):
                t0 = j * TT + sub * P
                postnorm_residual(sub, yb[sub], ybr[sub], hs[t0:t0 + P, :], hd[t0:t0 + P, :])
        phase_end(mark)

    stages = stop_after or 5
    attention_phase()
    merge_phase()
    if stages >= 2:
        ffn_phase(0, h1, h2)
    if stages >= 3:
        hgrn_phase(h2, h3)
    if stages >= 4:
        ffn_phase(1, h3, out)
    if stages < 4:
        src = {1: h1, 2: h2, 3: h3}[stages]
        for t0 in range(0, S, P):
            OP("sp", lambda e: e.dma_start(out=big[0][:, 0:D], in_=src[t0:t0 + P, :]), W=["big0"], dma="d_x")
            OP("sp", lambda e: e.dma_start(out=out[t0:t0 + P, :], in_=big[0][:, 0:D]), R=["big0"], dma="d_o")
    tk.barrier()
    return nc, tk


def make_in_maps(inputs, n_b):
    c = host_constants()
    g = np.ascontiguousarray(np.asarray(inputs["norm_gains"], np.float32).reshape(8, -1))
    maps = []
    for b in range(n_b):
        m = {
            "x": np.ascontiguousarray(np.asarray(inputs["x"][b], np.float32)),
            "gains": g,
            "rel_bias": np.asarray(inputs["rel_bias"], np.float32),
            "a_win": np.asarray(inputs["attn_w_in"][0], np.float32),
            "a_wout": np.asarray(inputs["attn_w_out"][0], np.float32),
            "h_win": np.asarray(inputs["hgrn_w_in"][0], np.float32),
            "lb_log": np.asarray(inputs["hgrn_lb_logits"], np.float32),
            "h_og": np.asarray(inputs["hgrn_out_gain"], np.float32),
            "h_wout": np.asarray(inputs["hgrn_w_out"][0], np.float32),
            "f_win0": np.asarray(inputs["ffn_w_in"][0], np.float32),
            "f_win1": np.asarray(inputs["ffn_w_in"][1], np.float32),
            "f_wout0": np.asarray(inputs["ffn_w_out"][0], np.float32),
            "f_wout1": np.asarray(inputs["ffn_w_out"][1], np.float32),
        }
        m.update(c)
        maps.append(m)
    return maps


def kernel(**inputs):
    x = np.asarray(inputs["x"])
    B, S, D = x.shape
    DFF = np.asarray(inputs["ffn_w_out"]).shape[1]
    nc, tk = build(S, D, DFF)
    maps = make_in_maps(inputs, B)
    res = run_bass_kernel_spmd(nc, maps, core_ids=list(range(B)))
    return np.stack([np.asarray(r["out"], np.float32) for r in res.results], axis=0)
```

```python
import math
import numpy as np
import concourse.bass as bass
import concourse.mybir as mybir
from concourse.bass_utils import run_bass_kernel_spmd

F32 = mybir.dt.float32
BF16 = mybir.dt.bfloat16
AF = mybir.ActivationFunctionType
ALU = mybir.AluOpType
AX = mybir.AxisListType
P = 128
NEG = -30000.0
EPS = 1e-6
GROUPS = ((128, 1), (512, 4), (2048, 16))
AH = 16
AD = AH * P


class Trk:
    def __init__(self, nc):
        self.nc = nc
        self.eng = {"pe": nc.tensor, "dve": nc.vector, "act": nc.scalar, "pool": nc.gpsimd, "sp": nc.sync}
        self.sems, self.cnt, self.waited, self.res = {}, {}, {}, {}
        self.nins = 0

    def sem(self, k):
        if k not in self.sems:
            self.sems[k] = self.nc.semaphore("s_" + k).__enter__()
        return self.sems[k]

    def op(self, e, fn, R=(), W=(), dma=None, after=()):
        evs = {}
        for k, v in after:
            evs[k] = max(evs.get(k, 0), v)
        for r in R:
            rec = self.res.get(r)
            if rec and rec[0]:
                k, v = rec[0]
                evs[k] = max(evs.get(k, 0), v)
        for w in W:
            rec = self.res.get(w)
            if rec:
                if rec[0]:
                    k, v = rec[0]
                    evs[k] = max(evs.get(k, 0), v)
                for k, v in rec[1].items():
                    evs[k] = max(evs.get(k, 0), v)
        need = [(k, v) for k, v in evs.items() if self.waited.get((e, k), 0) < v]
        for k, v in need[:-1]:
            self.eng[e].wait_ge(self.sem(k), v)
            self.waited[(e, k)] = v
        ins = fn(self.eng[e])
        if need:
            k, v = need[-1]
            ins._wait_ge(self.sem(k), v)
            self.waited[(e, k)] = v
        k, inc = (dma, 16) if dma else (e, 1)
        self.cnt[k] = self.cnt.get(k, 0) + inc
        ins.then_inc(self.sem(k), inc)
        ev = (k, self.cnt[k])
        for r in R:
            self.res.setdefault(r, [None, {}])[1][k] = ev[1]
        for w in W:
            self.res[w] = [ev, {}]
        self.nins += 1
        return ev

    def barrier(self):
        for e in self.eng:
            for k, v in self.cnt.items():
                if self.waited.get((e, k), 0) < v:
                    self.eng[e].wait_ge(self.sem(k), v)
                    self.waited[(e, k)] = v
        self.res = {}


def t5_bucket_np(dist):
    dist = np.maximum(np.asarray(dist, np.int64), 0)
    lr = np.log(np.maximum(dist, 1).astype(np.float32) / np.float32(16)) / np.float32(math.log(2048 / 16))
    large = 16 + (lr.astype(np.float32) * np.float32(16)).astype(np.int32)
    large = np.minimum(large, 31)
    return np.where(dist < 16, dist, large)


def host_constants():
    qi = np.arange(128)[:, None]
    ki = np.arange(256)[None, :]
    rel = qi - ki + 128
    valid = (rel >= 0) & (rel <= 128)
    oh = np.zeros((3, 33, 128 * 256), np.float32)
    for g, (w, d) in enumerate(GROUPS):
        b = t5_bucket_np(rel * d)
        for bb in range(32):
            oh[g, bb] = ((b == bb) & valid).reshape(-1)
        oh[g, 32] = np.where(valid, 0.0, NEG).reshape(-1)
    t = np.arange(128)
    triu = ((t[:, None] <= t[None, :]) & ((t[:, None] // 64) == (t[None, :] // 64))).astype(np.float32)
    return {"c_oh": oh, "c_ident": np.eye(128, dtype=np.float32), "c_triu": triu}


def build(S, D, DFF, TT=256, stop_after=None):
    KC = D // P
    NH = D // P
    NSUB = TT // P
    NT = S // TT
    FT = DFF // P
    nc = bass.Bass("TRN2", target_bir_lowering=False)
    tk = Trk(nc)
    OP = tk.op

    def din(name, shape):
        return nc.dram_tensor(name, list(shape), F32, kind="ExternalInput")

    x = din("x", [S, D]); gains = din("gains", [8, D]); rel_bias = din("rel_bias", [32, 48])
    a_win = din("a_win", [D, 9 * AD]); a_wout = din("a_wout", [AD, D])
    h_win = din("h_win", [D, 4 * D]); lb_log = din("lb_log", [2, D]); h_og = din("h_og", [1, P])
    h_wout = din("h_wout", [D, D])
    f_win = [din("f_win0", [D, 2 * DFF]), din("f_win1", [D, 2 * DFF])]
    f_wout = [din("f_wout0", [DFF, D]), din("f_wout1", [DFF, D])]
    c_oh = din("c_oh", [3, 33, 128 * 256]); c_ident = din("c_ident", [P, P]); c_triu = din("c_triu", [P, P])
    out = nc.dram_tensor("out", [S, D], F32, kind="ExternalOutput")
    h1 = nc.dram_tensor("h1", [S, D], F32); h2 = nc.dram_tensor("h2", [S, D], F32); h3 = nc.dram_tensor("h3", [S, D], F32)
    o_scr = nc.dram_tensor("o_scr", [3, S, AD], F32); l_scr = nc.dram_tensor("l_scr", [3, S, AH], F32)
    b_scr = nc.dram_tensor("b_scr", [3, 48, 128 * 256], F32)

    ctxs = []

    uniq = [0]

    def sb(name, shape, dt):
        uniq[0] += 1
        c = nc.sbuf_tensor(f"{name}_{uniq[0]}", list(shape), dt)
        t = c.__enter__()
        ctxs.append(c)
        return t

    def phase_begin():
        return len(ctxs)

    def phase_end(mark):
        tk.barrier()
        if len(wbuf) > 2:
            wbuf.pop()
        nwb[0] = 2
        while len(ctxs) > mark:
            ctxs.pop().__exit__(None, None, None)

    ident_f = sb("ident_f", [P, P], F32); ident_b = sb("ident_b", [P, P], BF16)
    triu_f = sb("triu_f", [P, P], F32)
    gainT = sb("gainT", [P, 8, KC], F32)
    gain_b = sb("gain_b", [P, D], F32)
    big = [sb(f"big{i}", [P, max(D, AD)], F32) for i in range(3)]
    ybf = sb("ybf", [P, D], BF16)
    YC = max(KC, 16)
    yT = sb("yT", [P, YC, TT], BF16)
    WBE = 8192
    wbuf = [sb(f"wbuf{i}", [P, WBE], BF16) for i in range(2)]
    st = sb("st", [P, 8], F32)
    psf = [nc.psum_tensor(f"psf{i}", [P, 512], F32).__enter__() for i in range(6)]
    psb = [nc.psum_tensor(f"psb{i}", [P, 1024], BF16).__enter__() for i in range(2)]
    psi = [0, 0]

    def npf():
        psi[0] = (psi[0] + 1) % 6
        return psi[0]

    def npb():
        psi[1] = (psi[1] + 1) % 2
        return psi[1]

    OP("sp", lambda e: e.dma_start(out=ident_f[:], in_=c_ident[:, :]), W=["ident_f"], dma="d_c1")
    OP("sp", lambda e: e.dma_start(out=triu_f[:], in_=c_triu[:, :]), W=["triu_f"], dma="d_c2")
    OP("sp", lambda e: e.dma_start(out=gainT[:], in_=gains.ap().rearrange("r (c p) -> p r c", p=P),
                                   allow_slow_non_contiguous=True), W=["gainT"], dma="d_c3")
    OP("dve", lambda e: e.tensor_copy(out=ident_b[:], in_=ident_f[:]), R=["ident_f"], W=["ident_b"])

    WA, WB = {}, {}
    prep_n = [0]
    cast_i = [0]
    cast_last = [None, None]

    def cast_dma(fn):
        sl = cast_i[0] % 2
        cast_i[0] += 1
        cast_last[sl] = OP("pool", fn, W=[f"castslot{sl}"], dma=f"d_cs{sl}")
        return tuple(ev for ev in cast_last if ev is not None)

    def per_of(KCn, ntiles_call):
        return min(max(1, WBE // (KCn * P)), ntiles_call)

    KPER = WBE // 512

    def prep_fm(wd, base, ntiles, KCn, per):
        gw = per * P
        ng = (ntiles + per - 1) // per
        prep_n[0] += 1
        t = nc.dram_tensor(f"{wd.name}_A{prep_n[0]}", [ng, P, KCn * gw], BF16)
        semk = f"d_cA{prep_n[0]}"
        last = None
        for g_ in range(ng):
            gwn = min(gw, ntiles * P - g_ * gw)
            c0 = base + g_ * gw
            last = cast_dma(lambda e, g_=g_, gwn=gwn, c0=c0: e.dma_start(
                out=t[g_, :, 0:KCn * gwn].rearrange("p (c n) -> p c n", n=gwn),
                in_=wd[0:KCn * P, c0:c0 + gwn].rearrange("(c p) n -> p c n", p=P)))
        WA.setdefault(wd.name, []).append((base, ntiles * P, gw, t, last))

    def prep_tm(wd, base, ncols_total, KCn):
        ncg = ncols_total // 512
        nkg = (KCn + KPER - 1) // KPER
        prep_n[0] += 1
        t = nc.dram_tensor(f"{wd.name}_B{prep_n[0]}", [ncg, nkg, P, KPER * 512], BF16)
        semk = f"d_cB{prep_n[0]}"
        last = None
        for cg in range(ncg):
            for kg in range(nkg):
                kn = min(KPER, KCn - kg * KPER)
                last = cast_dma(lambda e, cg=cg, kg=kg, kn=kn: e.dma_start(
                    out=t[cg, kg, :, 0:kn * 512].rearrange("p (c n) -> p c n", n=512),
                    in_=wd[kg * KPER * P:(kg * KPER + kn) * P, base + cg * 512:base + (cg + 1) * 512].rearrange("(c p) n -> p c n", p=P)))
        WB.setdefault(wd.name, []).append((base, ncols_total, t, last))

    for g_ in range(3):
        prep_fm(a_win, g_ * 3 * AD, 2 * AH, KC, per_of(KC, AH))
        prep_tm(a_win, g_ * 3 * AD + 2 * AD, AD, KC)
    prep_tm(a_wout, 0, D, AH)
    prep_fm(f_win[0], 0, 2 * FT, KC, per_of(KC, FT)); prep_tm(f_wout[0], 0, D, FT)
    prep_fm(h_win, 0, 2 * NH, KC, 1); prep_tm(h_win, 2 * D, 2 * D, KC); prep_tm(h_wout, 0, D, KC)
    prep_fm(f_win[1], 0, 2 * FT, KC, per_of(KC, FT)); prep_tm(f_wout[1], 0, D, FT)

    wq = [0]
    nwb = [2]

    def load_fm(wd, n0, tn, KCn):
        for (base, ncl, gw, t, res) in WA[wd.name]:
            if base <= n0 < base + ncl:
                break
        else:
            raise KeyError((wd.name, n0))
        assert (n0 - base) % gw == 0 and tn * P <= gw
        gi = (n0 - base) // gw
        i = wq[0] % nwb[0]
        wq[0] += 1
        ne = KCn * tn * P
        OP("sp", lambda e: e.dma_start(out=wbuf[i][:, 0:ne], in_=t[gi, :, 0:ne]), after=res, W=[f"wbuf{i}"], dma=f"d_w{i}")
        return i, wbuf[i][:, 0:ne].rearrange("p (c n) -> p c n", n=tn * P)

    def load_tm(wd, k0, kn, n0):
        for (base, ncl, t, res) in WB[wd.name]:
            if base <= n0 < base + ncl:
                break
        else:
            raise KeyError((wd.name, n0))
        assert (n0 - base) % 512 == 0 and k0 % KPER == 0
        i = wq[0] % nwb[0]
        wq[0] += 1
        ne = kn * 512
        OP("sp", lambda e: e.dma_start(out=wbuf[i][:, 0:ne], in_=t[(n0 - base) // 512, k0 // KPER, :, 0:ne]), after=res, W=[f"wbuf{i}"], dma=f"d_w{i}")
        return i, wbuf[i][:, 0:ne].rearrange("p (c n) -> p c n", n=512)

    def load_gain_b(row):
        OP("sp", lambda e: e.dma_start(out=gain_b[:], in_=gains[row:row + 1, :].partition_broadcast(P)),
           W=["gain_b"], dma="d_gb")

    ev_flip = [0]

    def evac(out_ap, in_ap, R, W, scale=None):
        ev_flip[0] ^= 1
        if ev_flip[0]:
            if scale is None:
                OP("act", lambda e: e.copy(out=out_ap, in_=in_ap), R=R, W=W)
            else:
                OP("act", lambda e: e.activation(out=out_ap, in_=in_ap, func=AF.Copy, scale=scale), R=R, W=W)
        else:
            if scale is None:
                OP("dve", lambda e: e.tensor_copy(out=out_ap, in_=in_ap), R=R, W=W)
            else:
                OP("dve", lambda e: e.tensor_scalar(out=out_ap, in0=in_ap, scalar1=scale, scalar2=None, op0=ALU.mult), R=R, W=W)

    def rstd_of(src_ap, src_res, n_feat, col=0):
        OP("act", lambda e: e.activation(out=ybf[:, 0:n_feat], in_=src_ap, func=AF.Square, accum_out=st[:, 6:7]),
           R=src_res, W=["ybf", "st6"])
        OP("dve", lambda e: e.tensor_scalar(out=st[:, 7:8], in0=st[:, 6:7], scalar1=1.0 / n_feat, scalar2=EPS,
                                            op0=ALU.mult, op1=ALU.add), R=["st6"], W=["st7"])
        OP("act", lambda e: e.activation(out=st[:, 7:8], in_=st[:, 7:8], func=AF.Sqrt), R=["st7"], W=["st7"])
        OP("dve", lambda e: e.reciprocal(out=st[:, col:col + 1], in_=st[:, 7:8]), R=["st7"], W=[f"st{col}"])

    def prenorm_T(rows_ap, sub, grow, dstT=None, KCn=None):
        OP("sp", lambda e: e.dma_start(out=big[0][:, 0:D], in_=rows_ap), W=["big0"], dma="d_x")
        rstd_of(big[0][:, 0:D], ["big0"], D, col=0)
        OP("dve", lambda e: e.tensor_scalar(out=ybf[:], in0=big[0][:, 0:D], scalar1=st[:, 0:1], scalar2=None, op0=ALU.mult),
           R=["big0", "st0"], W=["ybf"])
        transpose_into(ybf, "ybf", KC, sub, grow)

    def transpose_into(src, src_res, kcn, sub, grow=None):
        for k0 in range(0, kcn, 8):
            kn = min(8, kcn - k0)
            b = npb()
            for j in range(kn):
                OP("pe", lambda e, j=j: e.transpose(out=psb[b][:, j * P:(j + 1) * P], in_=src[:, (k0 + j) * P:(k0 + j + 1) * P],
                                                    identity=ident_b[:]), R=[src_res, "ident_b"], W=[f"psb{b}"])
            for j in range(kn):
                kc = k0 + j
                dst = yT[:, kc, sub * P:(sub + 1) * P]
                if grow is None:
                    evac(dst, psb[b][:, j * P:(j + 1) * P], [f"psb{b}"], ["yT"])
                else:
                    evac(dst, psb[b][:, j * P:(j + 1) * P], [f"psb{b}", "gainT"], ["yT"], scale=gainT[:, grow, kc:kc + 1])

    def dense_fm(wd, col0, ntiles, KCn, rhs_fn, rhs_res, T, evac_fn, pair_offset=None):
        per = per_of(KCn, ntiles)
        for t0 in range(0, ntiles, per):
            tn = min(per, ntiles - t0)
            i, wv = load_fm(wd, col0 + t0 * P, tn, KCn)
            for tl in range(tn):
                b = npf()
                for kc in range(KCn):
                    OP("pe", lambda e, kc=kc, tl=tl: e.matmul(psf[b][:, 0:T], lhsT=wv[:, kc, tl * P:(tl + 1) * P], rhs=rhs_fn(kc),
                                                              start=(kc == 0), stop=(kc == KCn - 1)),
                       R=[f"wbuf{i}"] + rhs_res, W=[f"psf{b}"])
                evac_fn(t0 + tl, b)

    def dense_tm(wd, col0, ncols, KCn, lhsT_fn, lhs_res, nsub, evac_fn):
        assert ncols == 512
        kper = KPER
        bs = [npf() for _ in range(nsub)]
        for k0 in range(0, KCn, kper):
            kn = min(kper, KCn - k0)
            i, wv = load_tm(wd, k0, kn, col0)
            for s_ in range(nsub):
                for kl in range(kn):
                    kc = k0 + kl
                    OP("pe", lambda e, kc=kc, kl=kl, s_=s_: e.matmul(psf[bs[s_]][:, 0:ncols], lhsT=lhsT_fn(kc, s_), rhs=wv[:, kl, :],
                                                                      start=(kc == 0), stop=(kc == KCn - 1)),
                       R=[f"wbuf{i}"] + lhs_res, W=[f"psf{bs[s_]}"])
        for s_ in range(nsub):
            evac_fn(s_, bs[s_])

    def postnorm_residual(sub, ybuf, ybuf_res, res_rows_ap, dst_rows_ap, n_feat=None):
        rstd_of(ybuf[:, 0:D], [ybuf_res], D, col=1)
        OP("dve", lambda e: e.scalar_tensor_tensor(out=ybuf[:, 0:D], in0=ybuf[:, 0:D], scalar=st[:, 1:2], in1=gain_b[:],
                                                   op0=ALU.mult, op1=ALU.mult), R=[ybuf_res, "st1", "gain_b"], W=[ybuf_res])
        OP("sp", lambda e: e.dma_start(out=big[0][:, 0:D], in_=res_rows_ap), W=["big0"], dma="d_x")
        OP("dve", lambda e: e.tensor_tensor(out=ybuf[:, 0:D], in0=ybuf[:, 0:D], in1=big[0][:, 0:D], op=ALU.add), R=[ybuf_res, "big0"], W=[ybuf_res])
        OP("sp", lambda e: e.dma_start(out=dst_rows_ap, in_=ybuf[:, 0:D]), R=[ybuf_res], dma="d_st_" + ybuf_res)

    def attention_phase():
        mark = phase_begin()
        wbuf.append(sb("wbuf2", [P, WBE], BF16)); nwb[0] = 3
        rb33 = sb("rb33", [33, 48], F32)
        ohs = [sb(f"ohs{i}", [33, 512], F32) for i in range(2)]
        bst = [sb(f"bst{i}", [48, 512], F32) for i in range(2)]
        bias_sb = sb("bias_sb", [P, AH, 256], F32)
        QT = sb("QT", [P, AH, TT], BF16)
        KT = sb("KT", [P, AH, TT + P], BF16)
        V = sb("V", [P, NSUB + 1, AD], BF16)
        Ssb = [sb(f"Ssb{i}", [P, 256], F32) for i in range(2)]
        Pb = [sb(f"Pb{i}", [P, 256], BF16) for i in range(2)]
        PT = [sb(f"PT{i}", [P, 2, P], BF16) for i in range(2)]
        hst = sb("hst", [P, AH, 4], F32)
        Lsb = sb("Lsb", [P, AH], F32)
        OP("sp", lambda e: e.dma_start(out=rb33[0:32, :], in_=rel_bias[:, :]), W=["rb33a"], dma="d_rb")
        OP("dve", lambda e: e.memset(rb33[32:33, :], 1.0), W=["rb33b"])
        xr = [x.ap().rearrange("(m d) f -> d m f", d=d) for (_, d) in GROUPS]
        for g, (w_, d) in enumerate(GROUPS):
            Sd = S // d
            osr = o_scr[g].rearrange("(m d) f -> d m f", d=d)
            lsr = l_scr[g].rearrange("(m d) f -> d m f", d=d)
            for c in range(64):
                i = c % 2
                OP("sp", lambda e: e.dma_start(out=ohs[i][:], in_=c_oh[g, :, c * 512:(c + 1) * 512]), W=[f"ohs{i}"], dma=f"d_oh{i}")
                b = npf()
                OP("pe", lambda e: e.matmul(psf[b][0:48, 0:512], lhsT=rb33[:, :], rhs=ohs[i][:, :], start=True, stop=True),
                   R=["rb33a", "rb33b", f"ohs{i}"], W=[f"psf{b}"])
                evac(bst[i][:], psf[b][0:48, 0:512], [f"psf{b}"], [f"bst{i}"])
                OP("sp", lambda e: e.dma_start(out=b_scr[g, :, c * 512:(c + 1) * 512], in_=bst[i][:]), R=[f"bst{i}"], W=["b_scr"], dma=f"d_bs{i}")
            OP("sp", lambda e: e.dma_start(out=bias_sb[:], in_=b_scr[g, g * AH:(g + 1) * AH, :].rearrange("h (q k) -> q h k", k=256)),
               R=["b_scr"], W=["bias_sb"], dma="d_bias")
            for j in range(NT):
                r = (j * TT) // Sd
                m0 = (j * TT) % Sd
                first = (m0 == 0)
                for sub in range(NSUB):
                    prenorm_T(xr[g][r, m0 + sub * P:m0 + (sub + 1) * P, :], sub, 0)
                cb = g * 3 * AD

                def ev_q(t, b):
                    evac(QT[:, t, :], psf[b][:, 0:TT], [f"psf{b}"], ["QT"], scale=float(P) ** -0.5)

                def ev_k(t, b):
                    evac(KT[:, t, P:P + TT], psf[b][:, 0:TT], [f"psf{b}"], ["KT"])
                dense_fm(a_win, cb, AH, KC, lambda kc: yT[:, kc, :], ["yT"], TT, ev_q)
                dense_fm(a_win, cb + AD, AH, KC, lambda kc: yT[:, kc, :], ["yT"], TT, ev_k)
                for c in range(AD // 512):
                    def ev_v(s_, b, c=c):
                        evac(V[:, 1 + s_, c * 512:(c + 1) * 512], psf[b][:, 0:512], [f"psf{b}"], ["V"])
                    dense_tm(a_win, cb + 2 * AD + c * 512, 512, KC, lambda kc, s_: yT[:, kc, s_ * P:(s_ + 1) * P], ["yT"], NSUB, ev_v)
                for blk in range(NSUB):
                    if first and blk == 0:
                        nk, ko = P, P
                    else:
                        nk, ko = 2 * P, 0
                    for h in range(AH):
                        u = h % 2
                        b = npf()
                        OP("pe", lambda e: e.matmul(psf[b][:, 0:nk], lhsT=QT[:, h, blk * P:(blk + 1) * P],
                                                    rhs=KT[:, h, blk * P + ko:blk * P + ko + nk], start=True, stop=True),
                           R=["QT", "KT"], W=[f"psf{b}"])
                        OP("dve", lambda e: e.tensor_tensor(out=Ssb[u][:, 0:nk], in0=psf[b][:, 0:nk], in1=bias_sb[:, h, ko:ko + nk], op=ALU.add),
                           R=[f"psf{b}", "bias_sb"], W=[f"Ssb{u}"])
                        OP("dve", lambda e: e.tensor_reduce(out=hst[:, h, 0:1], in_=Ssb[u][:, 0:nk], axis=AX.X, op=ALU.max),
                           R=[f"Ssb{u}"], W=[f"hs0_{h}"])
                        OP("dve", lambda e: e.tensor_scalar(out=hst[:, h, 1:2], in0=hst[:, h, 0:1], scalar1=-1.0, scalar2=None, op0=ALU.mult),
                           R=[f"hs0_{h}"], W=[f"hs1_{h}"])
                        OP("act", lambda e: e.activation(out=Pb[u][:, 0:nk], in_=Ssb[u][:, 0:nk], func=AF.Exp, bias=hst[:, h, 1:2],
                                                         accum_out=hst[:, h, 2:3]), R=[f"Ssb{u}", f"hs1_{h}"], W=[f"Pb{u}", f"hs2_{h}"])
                        nkb = nk // P
                        bb = npb()
                        for kb in range(nkb):
                            OP("pe", lambda e, kb=kb: e.transpose(out=psb[bb][:, kb * P:(kb + 1) * P], in_=Pb[u][:, kb * P:(kb + 1) * P],
                                                                  identity=ident_b[:]), R=[f"Pb{u}", "ident_b"], W=[f"psb{bb}"])
                        OP("dve", lambda e: e.tensor_copy(out=PT[u][:, 0:nkb, :], in_=psb[bb][:, 0:nkb * P].rearrange("p (a b) -> p a b", b=P)),
                           R=[f"psb{bb}"], W=[f"PT{u}"])
                        b2 = npf()
                        for kb in range(nkb):
                            OP("pe", lambda e, kb=kb: e.matmul(psf[b2][:, 0:P], lhsT=PT[u][:, kb, :],
                                                                rhs=V[:, blk + ko // P + kb, h * P:(h + 1) * P], start=(kb == 0), stop=(kb == nkb - 1)),
                               R=[f"PT{u}", "V"], W=[f"psf{b2}"])
                        OP("dve", lambda e: e.reciprocal(out=hst[:, h, 3:4], in_=hst[:, h, 2:3]), R=[f"hs2_{h}"], W=[f"hs3_{h}"])
                        OP("act", lambda e: e.activation(out=big[1][:, h * P:(h + 1) * P], in_=psf[b2][:, 0:P], func=AF.Copy, scale=hst[:, h, 3:4]),
                           R=[f"psf{b2}", f"hs3_{h}"], W=["big1"])
                        OP("act", lambda e: e.activation(out=Lsb[:, h:h + 1], in_=hst[:, h, 2:3], func=AF.Ln), R=[f"hs2_{h}"], W=["Lsb"])
                        OP("dve", lambda e: e.tensor_tensor(out=Lsb[:, h:h + 1], in0=Lsb[:, h:h + 1], in1=hst[:, h, 0:1], op=ALU.add),
                           R=["Lsb", f"hs0_{h}"], W=["Lsb"])
                    rows = slice(m0 + blk * P, m0 + (blk + 1) * P)
                    OP("sp", lambda e: e.dma_start(out=osr[r, rows, :], in_=big[1][:, 0:AD]), R=["big1"], W=["o_scr"], dma="d_os")
                    OP("sp", lambda e: e.dma_start(out=lsr[r, rows, :], in_=Lsb[:]), R=["Lsb"], W=["l_scr"], dma="d_ls")
                OP("act", lambda e: e.copy(out=KT[:, :, 0:P], in_=KT[:, :, TT:TT + P]), R=["KT"], W=["KT"])
                OP("act", lambda e: e.copy(out=V[:, 0, :], in_=V[:, NSUB, :]), R=["V"], W=["V"])
        phase_end(mark)

    def merge_phase():
        mark = phase_begin()
        wbuf.append(sb("wbuf2", [P, WBE], BF16)); nwb[0] = 3
        Lg = sb("Lg", [P, 3, AH], F32)
        al = sb("al", [P, 3, AH], F32)
        mx = sb("mx", [P, AH], F32)
        mrg = sb("mrg", [P, AD], F32)
        tmp = sb("tmp", [P, AD], F32)
        mbf = sb("mbf", [P, AD], BF16)
        load_gain_b(1)
        og0 = sb("og0", [P, AD], F32); og1 = sb("og1", [P, AD], F32); og2 = sb("og2", [P, AD], F32)
        Og = [og0[:], og1[:], og2[:]]
        for j in range(NT):
            for sub in range(NSUB):
                t0 = j * TT + sub * P
                for g in range(3):
                    OP("sp", lambda e, g=g: e.dma_start(out=Og[g], in_=o_scr[g, t0:t0 + P, :]), R=["o_scr"], W=[f"Og{g}"], dma=f"d_og{g}")
                OP("sp", lambda e: e.dma_start(out=Lg[:], in_=l_scr[:, t0:t0 + P, :].rearrange("g t h -> t g h")), R=["l_scr"], W=["Lg"], dma="d_lg")
                OP("dve", lambda e: e.tensor_tensor(out=mx[:], in0=Lg[:, 0, :], in1=Lg[:, 1, :], op=ALU.max), R=["Lg"], W=["mx"])
                OP("dve", lambda e: e.tensor_tensor(out=mx[:], in0=mx[:], in1=Lg[:, 2, :], op=ALU.max), R=["Lg", "mx"], W=["mx"])
                for g in range(3):
                    OP("dve", lambda e, g=g: e.tensor_tensor(out=al[:, g, :], in0=Lg[:, g, :], in1=mx[:], op=ALU.subtract), R=["Lg", "mx"], W=["al"])
                OP("act", lambda e: e.activation(out=al[:], in_=al[:], func=AF.Exp), R=["al"], W=["al"])
                OP("dve", lambda e: e.tensor_tensor(out=mx[:], in0=al[:, 0, :], in1=al[:, 1, :], op=ALU.add), R=["al"], W=["mx"])
                OP("dve", lambda e: e.tensor_tensor(out=mx[:], in0=mx[:], in1=al[:, 2, :], op=ALU.add), R=["al", "mx"], W=["mx"])
                OP("dve", lambda e: e.reciprocal(out=mx[:], in_=mx[:]), R=["mx"], W=["mx"])
                for g in range(3):
                    OP("dve", lambda e, g=g: e.tensor_tensor(out=al[:, g, :], in0=al[:, g, :], in1=mx[:], op=ALU.mult), R=["al", "mx"], W=["al"])

                def v3(ap):
                    return ap.rearrange("p (h d) -> p h d", d=P)

                def bc(g):
                    return al[:, g, :].unsqueeze(2).to_broadcast([P, AH, P])
                OP("dve", lambda e: e.tensor_tensor(out=v3(mrg[:]), in0=v3(Og[0]), in1=bc(0), op=ALU.mult), R=["Og0", "al"], W=["mrg"])
                OP("dve", lambda e: e.tensor_tensor(out=v3(tmp[:]), in0=v3(Og[1]), in1=bc(1), op=ALU.mult), R=["Og1", "al"], W=["tmp"])
                OP("dve", lambda e: e.tensor_tensor(out=mrg[:], in0=mrg[:], in1=tmp[:], op=ALU.add), R=["mrg", "tmp"], W=["mrg"])
                OP("dve", lambda e: e.tensor_tensor(out=v3(tmp[:]), in0=v3(Og[2]), in1=bc(2), op=ALU.mult), R=["Og2", "al"], W=["tmp"])
                OP("dve", lambda e: e.tensor_tensor(out=mbf[:], in0=mrg[:], in1=tmp[:], op=ALU.add), R=["mrg", "tmp"], W=["mbf"])
                transpose_into(mbf, "mbf", AH, sub)
            yb = [big[1], big[2]]
            ybr = ["big1", "big2"]
            for c in range(D // 512):
                def ev_y(s_, b, c=c):
                    evac(yb[s_][:, c * 512:(c + 1) * 512], psf[b][:, 0:512], [f"psf{b}"], [ybr[s_]])
                dense_tm(a_wout, c * 512, 512, AH, lambda kc, s_: yT[:, kc, s_ * P:(s_ + 1) * P], ["yT"], NSUB, ev_y)
            for sub in range(NSUB):
                t0 = j * TT + sub * P
                postnorm_residual(sub, yb[sub], ybr[sub], x[t0:t0 + P, :], h1[t0:t0 + P, :])
        phase_end(mark)

    def ffn_phase(l, hs, hd):
        mark = phase_begin()
        wbuf.append(sb("wbuf2", [P, WBE], BF16)); nwb[0] = 3
        HT = sb("HT", [P, FT, TT], BF16)
        sg = [sb(f"sg{i}", [P, TT], F32) for i in range(2)]
        load_gain_b(4 * l + 3)
        for j in range(NT):
            for sub in range(NSUB):
                t0 = j * TT + sub * P
                prenorm_T(hs[t0:t0 + P, :], sub, 4 * l + 2)
            per = per_of(KC, FT)
            for t0_ in range(0, FT, per):
                tn = min(per, FT - t0_)
                ig, wg = load_fm(f_win[l], t0_ * P, tn, KC)
                iu, wu = load_fm(f_win[l], DFF + t0_ * P, tn, KC)
                for tl in range(tn):
                    t = t0_ + tl
                    bg = npf(); bu = npf()
                    for kc in range(KC):
                        OP("pe", lambda e, kc=kc: e.matmul(psf[bg][:, 0:TT], lhsT=wg[:, kc, tl * P:(tl + 1) * P], rhs=yT[:, kc, :],
                                                           start=(kc == 0), stop=(kc == KC - 1)), R=[f"wbuf{ig}", "yT"], W=[f"psf{bg}"])
                    for kc in range(KC):
                        OP("pe", lambda e, kc=kc: e.matmul(psf[bu][:, 0:TT], lhsT=wu[:, kc, tl * P:(tl + 1) * P], rhs=yT[:, kc, :],
                                                           start=(kc == 0), stop=(kc == KC - 1)), R=[f"wbuf{iu}", "yT"], W=[f"psf{bu}"])
                    u = t % 2
                    OP("act", lambda e: e.activation(out=sg[u][:], in_=psf[bg][:, 0:TT], func=AF.Silu), R=[f"psf{bg}"], W=[f"sg{u}"])
                    OP("dve", lambda e: e.tensor_tensor(out=HT[:, t, :], in0=sg[u][:], in1=psf[bu][:, 0:TT], op=ALU.mult),
                       R=[f"sg{u}", f"psf{bu}"], W=["HT"])
            yb = [big[1], big[2]]
            ybr = ["big1", "big2"]
            for c in range(D // 512):
                def ev_y(s_, b, c=c):
                    evac(yb[s_][:, c * 512:(c + 1) * 512], psf[b][:, 0:512], [f"psf{b}"], [ybr[s_]])
                dense_tm(f_wout[l], c * 512, 512, FT, lambda kc, s_: HT[:, kc, s_ * P:(s_ + 1) * P], ["HT"], NSUB, ev_y)
            for sub in range(NSUB):
                t0 = j * TT + sub * P
                postnorm_residual(sub, yb[sub], ybr[sub], hs[t0:t0 + P, :], hd[t0:t0 + P, :])
        phase_end(mark)

    def hgrn_phase(hs, hd):
        mark = phase_begin()
        NCH = TT // 64
        lbt = sb("lbt", [P, 2, NH], F32)
        lbT = sb("lbT", [P, NH], F32); omlT = sb("omlT", [P, NH], F32)
        og_b = sb("og_b", [P, P], F32)
        ones = sb("ones", [P, 64], F32)
        S32 = sb("S32", [P, NH, P], F32); Sbf = sb("Sbf", [P, NH, P], BF16)
        V = sb("Vh", [P, NSUB, D], BF16)
        Ob = sb("Ob", [P, NSUB, D], BF16)
        ssq = sb("ssq", [P, NSUB, NH], F32)
        f32t = {n: sb("t_" + n, [P, TT], F32) for n in ("qs", "fg", "lf", "kT", "bT", "bm", "bl", "E1", "E2", "E3", "E4")}
        dec = sb("dec", [P, NCH], F32)
        qd = sb("qd", [P, TT], BF16); kd = sb("kd", [P, TT], BF16); klT = sb("klT", [P, TT], BF16)
        qbz = sb("qbz", [P, NCH, P], BF16)
        kl = sb("kl", [P, NSUB, P], BF16)
        ATm = sb("ATm", [P, P], BF16)
        junk = sb("junk", [P, P], BF16)
        Gs = sb("Gs", [P, 512], F32)
        tm3 = sb("tm3", [P, 512], F32)
        OP("sp", lambda e: e.dma_start(out=lbt[:], in_=lb_log.ap().rearrange("r (h p) -> p r h", p=P), allow_slow_non_contiguous=True),
           W=["lbt"], dma="d_lb")
        OP("sp", lambda e: e.dma_start(out=og_b[:], in_=h_og[0:1, :].partition_broadcast(P)), W=["og_b"], dma="d_og")
        OP("dve", lambda e: e.tensor_tensor(out=lbT[:], in0=lbt[:, 1, :], in1=lbt[:, 0, :], op=ALU.subtract), R=["lbt"], W=["lbT"])
        OP("act", lambda e: e.activation(out=lbT[:], in_=lbT[:], func=AF.Sigmoid), R=["lbT"], W=["lbT"])
        OP("dve", lambda e: e.tensor_scalar(out=omlT[:], in0=lbT[:], scalar1=-1.0, scalar2=1.0, op0=ALU.mult, op1=ALU.add), R=["lbT"], W=["omlT"])
        OP("dve", lambda e: e.memset(ones[:], 1.0), W=["ones"])
        OP("dve", lambda e: e.memset(S32[:], 0.0), W=["S32"])
        OP("dve", lambda e: e.memset(Sbf[:], 0.0), W=["Sbf"])
        OP("dve", lambda e: e.memset(qbz[:], 0.0), W=["qbz"])
        load_gain_b(5)
        T = f32t
        sc = float(P) ** -0.5

        def c3(ap):
            return ap.rearrange("p (c t) -> p c t", t=64)
        for j in range(NT):
            for sub in range(NSUB):
                t0 = j * TT + sub * P
                prenorm_T(hs[t0:t0 + P, :], sub, 4)
            for c in range(D // 512):
                def ev_v(s_, b, c=c):
                    evac(V[:, s_, c * 512:(c + 1) * 512], psf[b][:, 0:512], [f"psf{b}"], ["Vh"])
                dense_tm(h_win, 2 * D + c * 512, 512, KC, lambda kc, s_: yT[:, kc, s_ * P:(s_ + 1) * P], ["yT"], NSUB, ev_v)
            for h in range(NH):
                def ev_q(t, b):
                    OP("act", lambda e: e.activation(out=T["qs"][:], in_=psf[b][:, 0:TT], func=AF.Silu), R=[f"psf{b}"], W=["qs"])

                def ev_f(t, b):
                    OP("act", lambda e: e.activation(out=T["fg"][:], in_=psf[b][:, 0:TT], func=AF.Sigmoid), R=[f"psf{b}"], W=["fg"])
                dense_fm(h_win, h * P, 1, KC, lambda kc: yT[:, kc, :], ["yT"], TT, ev_q)
                dense_fm(h_win, D + h * P, 1, KC, lambda kc: yT[:, kc, :], ["yT"], TT, ev_f)
                OP("dve", lambda e: e.tensor_scalar(out=T["fg"][:], in0=T["fg"][:], scalar1=omlT[:, h:h + 1], scalar2=lbT[:, h:h + 1],
                                                    op0=ALU.mult, op1=ALU.add), R=["fg", "omlT", "lbT"], W=["fg"])
                OP("act", lambda e: e.activation(out=T["lf"][:], in_=T["fg"][:], func=AF.Ln), R=["fg"], W=["lf"])
                OP("dve", lambda e: e.tensor_scalar(out=T["kT"][:], in0=T["fg"][:], scalar1=-1.0, scalar2=1.0, op0=ALU.mult, op1=ALU.add),
                   R=["fg"], W=["kT"])
                for c in range(NCH):
                    OP("dve", lambda e, c=c: e.tensor_tensor_scan(out=T["bT"][:, c * 64:(c + 1) * 64], data0=ones[:], data1=T["lf"][:, c * 64:(c + 1) * 64],
                                                                   initial=0.0, op0=ALU.mult, op1=ALU.add), R=["lf", "ones"], W=["bT"])
                OP("dve", lambda e: e.tensor_tensor(out=c3(T["bm"][:]), in0=c3(T["bT"][:]), in1=c3(T["bT"][:])[:, :, 31:32].to_broadcast([P, NCH, 64]),
                                                    op=ALU.subtract), R=["bT"], W=["bm"])
                OP("dve", lambda e: e.tensor_tensor(out=c3(T["bl"][:]), in0=c3(T["bT"][:])[:, :, 63:64].to_broadcast([P, NCH, 64]), in1=c3(T["bT"][:]),
                                                    op=ALU.subtract), R=["bT"], W=["bl"])
                OP("act", lambda e: e.activation(out=T["E1"][:], in_=T["bm"][:], func=AF.Exp), R=["bm"], W=["E1"])
                OP("act", lambda e: e.activation(out=T["E2"][:], in_=T["bm"][:], func=AF.Exp, scale=-1.0), R=["bm"], W=["E2"])
                OP("act", lambda e: e.activation(out=T["E3"][:], in_=T["bT"][:], func=AF.Exp), R=["bT"], W=["E3"])
                OP("act", lambda e: e.activation(out=T["E4"][:], in_=T["bl"][:], func=AF.Exp), R=["bl"], W=["E4"])
                OP("act", lambda e: e.activation(out=dec[:].unsqueeze(2), in_=c3(T["bT"][:])[:, :, 63:64], func=AF.Exp), R=["bT"], W=["dec"])
                OP("dve", lambda e: e.scalar_tensor_tensor(out=qd[:], in0=T["qs"][:], scalar=sc, in1=T["E1"][:], op0=ALU.mult, op1=ALU.mult),
                   R=["qs", "E1"], W=["qd"])
                for c in range(NCH):
                    hf = c % 2
                    OP("dve", lambda e, c=c, hf=hf: e.scalar_tensor_tensor(out=qbz[:, c, hf * 64:(hf + 1) * 64], in0=T["qs"][:, c * 64:(c + 1) * 64], scalar=sc,
                                                                            in1=T["E3"][:, c * 64:(c + 1) * 64], op0=ALU.mult, op1=ALU.mult),
                       R=["qs", "E3"], W=["qbz"])
                OP("dve", lambda e: e.tensor_tensor(out=kd[:], in0=T["kT"][:], in1=T["E2"][:], op=ALU.mult), R=["kT", "E2"], W=["kd"])
                OP("dve", lambda e: e.tensor_tensor(out=klT[:], in0=T["kT"][:], in1=T["E4"][:], op=ALU.mult), R=["kT", "E4"], W=["klT"])
                bb = npb()
                for s_ in range(NSUB):
                    OP("pe", lambda e, s_=s_: e.transpose(out=psb[bb][:, s_ * P:(s_ + 1) * P], in_=klT[:, s_ * P:(s_ + 1) * P], identity=ident_b[:]),
                       R=["klT", "ident_b"], W=[f"psb{bb}"])
                OP("dve", lambda e: e.tensor_copy(out=kl[:], in_=psb[bb][:, 0:NSUB * P].rearrange("p (a b) -> p a b", b=P)), R=[f"psb{bb}"], W=["kl"])
                for s_ in range(NSUB):
                    ba = npf()
                    OP("pe", lambda e: e.matmul(psf[ba][:, 0:P], lhsT=kd[:, s_ * P:(s_ + 1) * P], rhs=qd[:, s_ * P:(s_ + 1) * P], start=True, stop=True),
                       R=["kd", "qd"], W=[f"psf{ba}"])
                    OP("dve", lambda e: e.tensor_tensor(out=ATm[:], in0=psf[ba][:, 0:P], in1=triu_f[:], op=ALU.mult), R=[f"psf{ba}", "triu_f"], W=["ATm"])
                    bo = npf()
                    OP("pe", lambda e: e.matmul(psf[bo][:, 0:P], lhsT=ATm[:], rhs=V[:, s_, h * P:(h + 1) * P], start=True, stop=False),
                       R=["ATm", "Vh"], W=[f"psf{bo}"])
                    for hf in range(2):
                        c = s_ * 2 + hf
                        OP("pe", lambda e, c=c, hf=hf: e.matmul(psf[bo][:, 0:P], lhsT=qbz[:, c, :], rhs=Sbf[:, h, :], start=False, stop=(hf == 1)),
                           R=["qbz", f"Sbf{h}"], W=[f"psf{bo}"])
                        bn = npf()
                        OP("pe", lambda e, hf=hf: e.matmul(psf[bn][:, 0:P], lhsT=kl[hf * 64:(hf + 1) * 64, s_, :], rhs=V[hf * 64:(hf + 1) * 64, s_, h * P:(h + 1) * P],
                                                           start=True, stop=True), R=["kl", "Vh"], W=[f"psf{bn}"])
                        OP("dve", lambda e, c=c: e.scalar_tensor_tensor(out=S32[:, h, :], in0=S32[:, h, :], scalar=dec[:, c:c + 1], in1=psf[bn][:, 0:P],
                                                                         op0=ALU.mult, op1=ALU.add), R=[f"S32{h}", "dec", f"psf{bn}"], W=[f"S32{h}"])
                        OP("act", lambda e: e.copy(out=Sbf[:, h, :], in_=S32[:, h, :]), R=[f"S32{h}"], W=[f"Sbf{h}"])
                    OP("act", lambda e: e.activation(out=junk[:], in_=psf[bo][:, 0:P], func=AF.Square, accum_out=ssq[:, s_, h:h + 1]),
                       R=[f"psf{bo}"], W=["junk", "ssq"])
                    OP("dve", lambda e: e.tensor_copy(out=Ob[:, s_, h * P:(h + 1) * P], in_=psf[bo][:, 0:P]), R=[f"psf{bo}"], W=["Ob"])
            OP("dve", lambda e: e.tensor_scalar(out=ssq[:], in0=ssq[:], scalar1=1.0 / P, scalar2=EPS, op0=ALU.mult, op1=ALU.add), R=["ssq"], W=["ssq"])
            OP("act", lambda e: e.activation(out=ssq[:], in_=ssq[:], func=AF.Sqrt), R=["ssq"], W=["ssq"])
            OP("dve", lambda e: e.reciprocal(out=ssq[:], in_=ssq[:]), R=["ssq"], W=["ssq"])
            for c in range(D // 512):
                def ev_g(s_, b, c=c):
                    hh = 512 // P
                    OP("act", lambda e: e.activation(out=Gs[:], in_=psf[b][:, 0:512], func=AF.Silu), R=[f"psf{b}"], W=["Gs"])
                    ov = Ob[:, s_, c * 512:(c + 1) * 512].rearrange("p (h d) -> p h d", d=P)
                    t3 = tm3[:].rearrange("p (h d) -> p h d", d=P)
                    OP("dve", lambda e: e.tensor_tensor(out=t3, in0=ov, in1=ssq[:, s_, c * hh:(c + 1) * hh].unsqueeze(2).to_broadcast([P, hh, P]), op=ALU.mult),
                       R=["Ob", "ssq"], W=["tm3"])
                    OP("dve", lambda e: e.tensor_tensor(out=t3, in0=t3, in1=og_b[:].unsqueeze(1).to_broadcast([P, hh, P]), op=ALU.mult),
                       R=["tm3", "og_b"], W=["tm3"])
                    OP("dve", lambda e: e.tensor_tensor(out=Ob[:, s_, c * 512:(c + 1) * 512], in0=tm3[:], in1=Gs[:], op=ALU.mult), R=["tm3", "Gs"], W=["Ob"])
                dense_tm(h_win, 3 * D + c * 512, 512, KC, lambda kc, s_: yT[:, kc, s_ * P:(s_ + 1) * P], ["yT"], NSUB, ev_g)
            for sub in range(NSUB):
                transpose_into(Ob[:, sub, :], "Ob", KC, sub)
            yb = [big[1], big[2]]
            ybr = ["big1", "big2"]
            for c in range(D // 512):
                def ev_y(s_, b, c=c):
                    evac(yb[s_][:, c * 512:(c + 1) * 512], psf[b][:, 0:512], [f"psf{b}"], [ybr[s_]])
                dense_tm(h_wout, c * 512, 512, KC, lambda kc, s_: yT[:, kc, s_ * P:(s_ + 1) * P], ["yT"], NSUB, ev_y)
            for sub in range(NSUB):
                t0 = j * TT + sub * P
                postnorm_residual(sub, yb[sub], ybr[sub], hs[t0:t0 + P, :], hd[t0:t0 + P, :])
        phase_end(mark)

    stages = stop_after or 5
    attention_phase()
    merge_phase()
    if stages >= 2:
        ffn_phase(0, h1, h2)
    if stages >= 3:
        hgrn_phase(h2, h3)
    if stages >= 4:
        ffn_phase(1, h3, out)
    if stages < 4:
        src = {1: h1, 2: h2, 3: h3}[stages]
        for t0 in range(0, S, P):
            OP("sp", lambda e: e.dma_start(out=big[0][:, 0:D], in_=src[t0:t0 + P, :]), W=["big0"], dma="d_x")
            OP("sp", lambda e: e.dma_start(out=out[t0:t0 + P, :], in_=big[0][:, 0:D]), R=["big0"], dma="d_o")
    tk.barrier()
    return nc, tk


def make_in_maps(inputs, n_b):
    c = host_constants()
    g = np.ascontiguousarray(np.asarray(inputs["norm_gains"], np.float32).reshape(8, -1))
    maps = []
    for b in range(n_b):
        m = {
            "x": np.ascontiguousarray(np.asarray(inputs["x"][b], np.float32)),
            "gains": g,
            "rel_bias": np.asarray(inputs["rel_bias"], np.float32),
            "a_win": np.asarray(inputs["attn_w_in"][0], np.float32),
            "a_wout": np.asarray(inputs["attn_w_out"][0], np.float32),
            "h_win": np.asarray(inputs["hgrn_w_in"][0], np.float32),
            "lb_log": np.asarray(inputs["hgrn_lb_logits"], np.float32),
            "h_og": np.asarray(inputs["hgrn_out_gain"], np.float32),
            "h_wout": np.asarray(inputs["hgrn_w_out"][0], np.float32),
            "f_win0": np.asarray(inputs["ffn_w_in"][0], np.float32),
            "f_win1": np.asarray(inputs["ffn_w_in"][1], np.float32),
            "f_wout0": np.asarray(inputs["ffn_w_out"][0], np.float32),
            "f_wout1": np.asarray(inputs["ffn_w_out"][1], np.float32),
        }
        m.update(c)
        maps.append(m)
    return maps


def kernel(**inputs):
    x = np.asarray(inputs["x"])
    B, S, D = x.shape
    DFF = np.asarray(inputs["ffn_w_out"]).shape[1]
    nc, tk = build(S, D, DFF)
    maps = make_in_maps(inputs, B)
    res = run_bass_kernel_spmd(nc, maps, core_ids=list(range(B)))
    return np.stack([np.asarray(r["out"], np.float32) for r in res.results], axis=0)
```
